# Optimizing a Trainium2 kernel written in Bass

```python
import jax, jax.numpy as jnp
from jax import lax
import numpy as np

D_MODEL = 2048
BATCH = 8
SEQ = 2048
DEPTH = 2
DEC_BATCH = 32
DEC_SEQ = 64
PAST_LEN = 2048

CHUNK = 64
D_POOL = D_MODEL // 2
POOL_WINDOWS = (2, 4, 8, 16)
N_POOL_GROUPS = len(POOL_WINDOWS)
POOL_GC = D_POOL // N_POOL_GROUPS
POOL_HIST = max(POOL_WINDOWS) - 1
HEAD_DIM = 64
N_HEADS = (D_MODEL - D_POOL) // HEAD_DIM
N_KV_HEADS = 4
GROUP = N_HEADS // N_KV_HEADS
WINDOW = 128
WIN_CHUNKS = WINDOW // CHUNK
D_Q = N_HEADS * HEAD_DIM
D_KV = N_KV_HEADS * HEAD_DIM
D_IN = D_POOL + D_Q + 2 * D_KV
D_FF = ((8 * D_MODEL + 3 * 256 - 1) // (3 * 256)) * 256
EPS = 1e-6
NEG = -1e30

kernel_name = "hymba_pool_swa_sink_stream_step"


def rmsnorm(x, g):
    xf = x.astype(jnp.float32)
    y = xf * lax.rsqrt(jnp.mean(xf * xf, axis=-1, keepdims=True) + EPS) * g.astype(jnp.float32)
    return y.astype(x.dtype)


def pool_mix(u, prev, pos0, pool_w, pool_scale):
    B, L, C = u.shape
    xp = jnp.concatenate([prev.astype(jnp.float32), u.astype(jnp.float32)], axis=1)
    cs = jnp.concatenate([jnp.zeros((B, 1, C), jnp.float32), jnp.cumsum(xp, axis=1)], axis=1)
    end = cs[:, POOL_HIST + 1:]
    pos = pos0 + jnp.arange(L)
    means = []
    for gi, w in enumerate(POOL_WINDOWS):
        sl = slice(gi * POOL_GC, (gi + 1) * POOL_GC)
        start = cs[:, POOL_HIST + 1 - w:POOL_HIST + 1 - w + L, sl]
        cnt = jnp.minimum(pos + 1, w).astype(jnp.float32)[None, :, None]
        means.append((end[..., sl] - start) / cnt)
    d = (jnp.concatenate(means, axis=-1) - u.astype(jnp.float32)).reshape(B, L, N_POOL_GROUPS, POOL_GC)
    y = jnp.einsum('blgc,gcd->blgd', d, pool_w.astype(jnp.float32)).reshape(B, L, C)
    y = y * pool_scale.astype(jnp.float32)
    return y.astype(u.dtype), xp[:, -POOL_HIST:].astype(u.dtype)


def sink_softmax(s, sinks):
    sk = jnp.broadcast_to(sinks.astype(jnp.float32).reshape(N_KV_HEADS, GROUP, 1, 1), s.shape[:-1] + (1,))
    return jax.nn.softmax(jnp.concatenate([s, sk], axis=-1), axis=-1)[..., :-1]


def attn_prompt(q, k, v, sinks):
    B, S = q.shape[:2]
    NC = S // CHUNK
    qc = q.reshape(B, NC, CHUNK, N_KV_HEADS, GROUP, HEAD_DIM)

    def band(t):
        tc = t.reshape(B, NC, CHUNK, N_KV_HEADS, HEAD_DIM)
        tp = jnp.concatenate([jnp.zeros((B, WIN_CHUNKS, CHUNK, N_KV_HEADS, HEAD_DIM), t.dtype), tc], axis=1)
        return jnp.concatenate([tp[:, i:i + NC] for i in range(WIN_CHUNKS + 1)], axis=2)

    kb, vb = band(k), band(v)
    s = jnp.einsum('bnqhgd,bnjhd->bnhgqj', qc, kb, preferred_element_type=jnp.float32) * (HEAD_DIM ** -0.5)
    J = (WIN_CHUNKS + 1) * CHUNK
    kchunk = jnp.arange(NC)[:, None] + (jnp.arange(J) // CHUNK)[None, :] - WIN_CHUNKS
    s = jnp.where((kchunk >= 0)[None, :, None, None, None, :], s, NEG)
    p = sink_softmax(s, sinks)
    o = jnp.einsum('bnhgqj,bnjhd->bnqhgd', p.astype(v.dtype), vb)
    return o.reshape(B, S, D_Q)


def attn_sample(q, k, v, k_past, v_past, sinks):
    B, T = q.shape[:2]
    kk = jnp.concatenate([k_past.astype(k.dtype), k], axis=1)
    vv = jnp.concatenate([v_past.astype(v.dtype), v], axis=1)
    qg = q.reshape(B, T, N_KV_HEADS, GROUP, HEAD_DIM)
    s = jnp.einsum('bqhgd,bjhd->bhgqj', qg, kk, preferred_element_type=jnp.float32) * (HEAD_DIM ** -0.5)
    p = sink_softmax(s, sinks)
    o = jnp.einsum('bhgqj,bjhd->bqhgd', p.astype(vv.dtype), vv).reshape(B, T, D_Q)
    return o, kk[:, -WINDOW:], vv[:, -WINDOW:]


def layer(x, pool_prev, pos0, k_past, v_past, g_mix_pre, w_in, pool_w, pool_scale, sinks,
          w_out, g_mix_post, g_ffn_pre, w_gate, w_up, w_down, g_ffn_post):
    B, L, _ = x.shape
    h = rmsnorm(x, g_mix_pre)
    z = h @ w_in
    u = z[..., :D_POOL]
    q = z[..., D_POOL:D_POOL + D_Q].reshape(B, L, N_HEADS, HEAD_DIM)
    k = z[..., D_POOL + D_Q:D_POOL + D_Q + D_KV].reshape(B, L, N_KV_HEADS, HEAD_DIM)
    v = z[..., D_POOL + D_Q + D_KV:].reshape(B, L, N_KV_HEADS, HEAD_DIM)
    pool_out, new_pool = pool_mix(u, pool_prev, pos0, pool_w, pool_scale)
    if k_past is None:
        attn_out = attn_prompt(q, k, v, sinks)
        new_k, new_v = k[:, -WINDOW:], v[:, -WINDOW:]
    else:
        attn_out, new_k, new_v = attn_sample(q, k, v, k_past, v_past, sinks)
    mix = jnp.concatenate([pool_out.astype(x.dtype), attn_out.astype(x.dtype)], axis=-1) @ w_out
    x = x + rmsnorm(mix, g_mix_post)
    h = rmsnorm(x, g_ffn_pre)
    f = (jax.nn.silu(h @ w_gate) * (h @ w_up)) @ w_down
    x = x + rmsnorm(f, g_ffn_post)
    return x, new_pool, new_k, new_v


def setup_inputs(seed: int = 0) -> dict:
    key = jax.random.key(seed)
    ks = jax.random.split(key, 20)
    n = jax.random.normal
    f32 = jnp.float32

    def gain(k):
        return 1.0 + 0.02 * n(k, (DEPTH, D_MODEL), f32)

    return {
        "x_prompt": n(ks[0], (BATCH, SEQ, D_MODEL), f32),
        "x_sample": n(ks[1], (DEC_BATCH, DEC_SEQ, D_MODEL), f32),
        "cache_k": n(ks[2], (DEPTH, DEC_BATCH, WINDOW, N_KV_HEADS, HEAD_DIM), f32),
        "cache_v": n(ks[3], (DEPTH, DEC_BATCH, WINDOW, N_KV_HEADS, HEAD_DIM), f32),
        "state_pool": n(ks[4], (DEPTH, DEC_BATCH, POOL_HIST, D_POOL), f32),
        "g_mix_pre": gain(ks[5]),
        "w_in": n(ks[6], (DEPTH, D_MODEL, D_IN), f32) * D_MODEL ** -0.5,
        "pool_w": n(ks[7], (DEPTH, N_POOL_GROUPS, POOL_GC, POOL_GC), f32) * POOL_GC ** -0.5,
        "pool_scale": 1.0 + 0.02 * n(ks[8], (DEPTH, D_POOL), f32),
        "attn_sinks": 0.5 * n(ks[9], (DEPTH, N_HEADS), f32),
        "w_out": n(ks[10], (DEPTH, D_MODEL, D_MODEL), f32) * D_MODEL ** -0.5,
        "g_mix_post": gain(ks[11]),
        "g_ffn_pre": gain(ks[12]),
        "w_gate": n(ks[13], (DEPTH, D_MODEL, D_FF), f32) * D_MODEL ** -0.5,
        "w_up": n(ks[14], (DEPTH, D_MODEL, D_FF), f32) * D_MODEL ** -0.5,
        "w_down": n(ks[15], (DEPTH, D_FF, D_MODEL), f32) * D_FF ** -0.5,
        "g_ffn_post": gain(ks[16]),
    }


def reference(x_prompt, x_sample, cache_k, cache_v, state_pool, g_mix_pre, w_in, pool_w, pool_scale,
              attn_sinks, w_out, g_mix_post, g_ffn_pre, w_gate, w_up, w_down, g_ffn_post):
    xp, xs = x_prompt, x_sample
    kp_l, vp_l, pp_l, ks_l, vs_l, ps_l = [], [], [], [], [], []
    for l in range(DEPTH):
        w = (g_mix_pre[l], w_in[l], pool_w[l], pool_scale[l], attn_sinks[l], w_out[l],
             g_mix_post[l], g_ffn_pre[l], w_gate[l], w_up[l], w_down[l], g_ffn_post[l])
        prev0 = jnp.zeros((xp.shape[0], POOL_HIST, D_POOL), xp.dtype)
        xp, pp, kp, vp = layer(xp, prev0, 0, None, None, *w)
        xs, ps, ks_, vs = layer(xs, state_pool[l], PAST_LEN, cache_k[l], cache_v[l], *w)
        kp_l.append(kp); vp_l.append(vp); pp_l.append(pp)
        ks_l.append(ks_); vs_l.append(vs); ps_l.append(ps)
    return (xp, xs, jnp.stack(kp_l), jnp.stack(vp_l), jnp.stack(pp_l),
            jnp.stack(ks_l), jnp.stack(vs_l), jnp.stack(ps_l))
```

```python
from contextlib import ExitStack

import numpy as np
import concourse.bass as bass
import concourse.mybir as mybir
from concourse.bass_utils import run_bass_kernel_spmd

F32 = mybir.dt.float32
BF16 = mybir.dt.bfloat16
AF = mybir.ActivationFunctionType
ALU = mybir.AluOpType

D = 2048
DFF = 5632
NL = 2
EPS = 1e-6
A_IDX = [0, 1, 2, 3, 8, 9, 10, 11]
POOL_WINS = (2, 4, 8, 16)

PC_GMP, PC_GMO, PC_GFP, PC_GFO = 0, 32, 64, 96
PC_PS = 128
PC_SK = 144
PC_RC = 176
PC_ID = 304
NPRM = 432


class Stream:
    def __init__(self, name, sem, selfsync):
        self.name, self.sem, self.selfsync = name, sem, selfsync
        self.cnt = 0
        self.ops = []
        self.waited = {}


class Ctx:
    def __init__(self):
        self.last_w = {}
        self.readers = {}
        self.dma_cnt = {}

    def job(self, st, reads, writes, fn, dma=None):
        writes = list(writes) + [k for k in reads if k[0] == "ps"]
        reads = [k for k in reads if k[0] != "ps"]
        toks = {}

        def add(tok):
            if tok is None:
                return
            k = id(tok[0])
            if k not in toks or toks[k][1] < tok[1]:
                toks[k] = tok

        for k in reads:
            add(self.last_w.get(k))
        for k in writes:
            add(self.last_w.get(k))
            for t in self.readers.get(k, {}).values():
                add(t)
        for s, v in toks.values():
            if s is st.sem and not st.selfsync:
                continue
            if st.waited.get(id(s), 0) >= v:
                continue
            st.waited[id(s)] = v
            st.ops.append(("w", s, v))
        if dma is None:
            st.cnt += 1
            tok = (st.sem, st.cnt)
            st.ops.append(("e", fn, True))
        else:
            sem, n = dma
            self.dma_cnt[id(sem)] = self.dma_cnt.get(id(sem), 0) + 16 * n
            tok = (sem, self.dma_cnt[id(sem)])
            st.ops.append(("e", fn, False))
        for k in reads:
            r = self.readers.setdefault(k, {})
            o = r.get(id(tok[0]))
            if o is None or o[1] < tok[1]:
                r[id(tok[0])] = tok
        for k in writes:
            self.last_w[k] = tok
            self.readers[k] = {}
        return tok


def _run(st, e):
    for op in st.ops:
        if op[0] == "w":
            e.wait_ge(op[1], op[2])
        else:
            ins = op[1](e)
            if op[2]:
                ins.then_inc(st.sem, 1)


def build_nc(tile_sel=None):
    nc = bass.Bass("TRN2", target_bir_lowering=False)

    def din(name, shape):
        return nc.dram_tensor(name, shape, F32, kind="ExternalInput").ap()

    def dout(name, shape):
        return nc.dram_tensor(name, shape, F32, kind="ExternalOutput").ap()

    def dscr(name, shape):
        return nc.dram_tensor(name, shape, BF16, kind="Internal").ap()

    xp = din("xp", [2048, D])
    xs = din("xs", [256, D])
    ck = din("ck", [2, 4, 128, 256])
    cv = din("cv", [2, 4, 128, 256])
    sp = din("sp", [2, 60, 1024])
    prm = din("prm", [128, NPRM])
    w_src = {
        "in": din("w_in", [2, D, 2560]),
        "pw": din("pool_w", [2, 1024, 256]),
        "out": din("w_out", [2, D, D]),
        "gate": din("w_gate", [2, D, DFF]),
        "up": din("w_up", [2, D, DFF]),
        "down": din("w_down", [2, DFF, D]),
    }
    w_bf = {
        "in": dscr("b_in", [2, D, 2560]),
        "pw": dscr("b_pw", [2, 1024, 256]),
        "out": dscr("b_out", [2, D, D]),
        "gate": dscr("b_gate", [2, D, DFF]),
        "up": dscr("b_up", [2, D, DFF]),
        "down": dscr("b_down", [2, DFF, D]),
    }
    yp = dout("yp", [2048, D])
    ys = dout("ys", [256, D])
    kpo = dout("kpo", [2, 128, 256])
    vpo = dout("vpo", [2, 128, 256])
    ppo = dout("ppo", [2, 15, 1024])
    kso = dout("kso", [2, 4, 128, 256])
    vso = dout("vso", [2, 4, 128, 256])
    pso = dout("pso", [2, 4, 15, 1024])

    es = ExitStack()

    def sb(name, shape, dt):
        return es.enter_context(nc.sbuf_tensor(name, shape, dt))

    def sem(name):
        return es.enter_context(nc.semaphore(name))

    xT = sb("xT", [128, 16, 512], F32)
    hn_raw = sb("hn", [128, 4096], F32)
    wsl = [sb(f"ws{i}", [128, 16, 512], BF16) for i in range(2)]
    bufA = sb("bufA", [128, 8192], F32)
    hid = sb("hid", [128, 44, 512], BF16)
    KT = sb("KT", [128, 2, 768], BF16)
    Vp = sb("Vp", [64, 12, 4, 128], BF16)
    PT = [sb(f"PT{i}", [64, 768], BF16) for i in range(3)]
    t1 = [sb("t1_0", [128, 256], F32)] * 2
    tmpP = sb("tmpP", [128, 512], F32)
    t2 = [sb(f"t2_{i}", [64, 256], F32) for i in range(2)]
    rstd = sb("rstd", [128, 512], F32)
    sg = [sb(f"sg{i}", [128, 512], F32) for i in range(4)]
    prmT = sb("prmT", [128, NPRM], F32)
    onesm = sb("onesm", [128, 128], BF16)
    esrow = sb("esrow", [64, 1024], BF16)
    zo = sb("zo", [64, 128], BF16)
    es16 = sb("es16", [128, 16], F32)
    zt = sb("zt", [128, 64], F32)
    KTh = [sb(f"KTh{l}", [128, 2, 128], BF16) for l in range(2)]
    Vh = [sb(f"Vh{l}", [64, 2, 4, 128], BF16) for l in range(2)]
    Uh = [sb(f"Uh{l}", [128, 8, 16], F32) for l in range(2)]
    ps = es.enter_context(nc.psum_tensor("ps", [128, 8, 512], F32))
    psf = ps[:, :, :].rearrange("p b t -> p (b t)")

    hn = hn_raw[:, :].bitcast(BF16).rearrange("p (c t) -> p c t", c=16)
    stg = [hn_raw[:, 0:2048], hn_raw[:, 2048:4096]]
    hid_f = hid[:, :, :].rearrange("p c t -> p (c t)").bitcast(F32)
    xstg = [hid_f[:, 0:2048], hid_f[:, 2048:4096], hid_f[:, 4096:6144], bufA[:, 0:2048]]
    xstg_keys = [[("H", b) for b in range(0, 8)], [("H", b) for b in range(8, 16)], [("H", b) for b in range(16, 24)],
                 [("A", b) for b in range(0, 4)]]
    U = bufA[:, 0:4224].rearrange("p (c e) -> p c e", c=8)
    tmpA = bufA[:, 4224:5280]
    tmpB = bufA[:, 5280:6336]
    kvst = bufA[0:64, 6656:7168]
    ust = bufA[0:64, 7168:8192]
    spst = bufA[0:60, 7168:8192]
    ckst = bufA[:, 2560:3584].rearrange("p (s f) -> p s f", s=4)
    mix = bufA[:, :].rearrange("p (c t) -> p c t", c=16)
    catT = hid[:, 0:16, :]
    dT = hid[:, 16:24, :]
    QT = hid[:, 24:32, :]
    sqH = hid[:, 28:44, :]
    ident = prmT[:, PC_ID:PC_ID + 128]

    st_pe = Stream("pe", sem("s_pe"), False)
    st_act = Stream("act", sem("s_act"), True)
    st_dve = Stream("dve", sem("s_dve"), True)
    st_gp = Stream("gp", sem("s_gp"), True)
    st_sp = Stream("sp", sem("s_sp"), False)
    sem_w = [sem("s_w0"), sem("s_w1")]
    sem_xl = [sem(f"s_xl{i}") for i in range(4)]
    sem_ys = [sem("s_ys0"), sem("s_ys1")]
    sem_misc = sem("s_misc")
    sem_ck = sem("s_ck")
    sem_cv = sem("s_cv")
    sem_sp = sem("s_spl")
    sem_kv = sem("s_kv")
    sem_u = sem("s_u")
    sem_cc = sem("s_cc")
    sem_castp = [sem("s_castp0"), sem("s_castp1")]

    cx = Ctx()
    J = cx.job

    def X(c):
        return ("x", c)

    def HN(c):
        return ("hn", c)

    def H(b):
        return ("H", b)

    def A(b):
        return ("A", b)

    def AK(lo, hi):
        return [A(b) for b in range(lo // 512, (hi - 1) // 512 + 1)]

    def PS(b):
        return ("ps", b)

    def W(s):
        return ("W", s)

    XALL = [X(c) for c in range(16)]
    HNALL = [HN(c) for c in range(16)]
    UKEYS = AK(0, 4224)

    bank_ctr = [0]

    def next_bank():
        b = bank_ctr[0] % 7
        bank_ctr[0] += 1
        return b

    slot_ctr = [0]

    def next_slot():
        s = slot_ctr[0] % 2
        slot_ctr[0] += 1
        return s

    def F_act(out, in_, func, **kw):
        return lambda e: e.activation(out=out, in_=in_, func=func, **kw)

    def F_tt(out, in0, in1, op):
        return lambda e: e.tensor_tensor(out=out, in0=in0, in1=in1, op=op)

    def F_stt(out, in0, scalar, in1, op0, op1):
        return lambda e: e.scalar_tensor_tensor(out=out, in0=in0, scalar=scalar, in1=in1, op0=op0, op1=op1)

    def F_copy(out, in_):
        return lambda e: e.tensor_copy(out=out, in_=in_)

    def F_memset(out, v):
        return lambda e: e.memset(out, v)

    def F_recip(out, in_):
        return lambda e: e.reciprocal(out=out, in_=in_)

    def F_mm(groups):
        def fn(e):
            ins = None
            for out, pairs, sf, sl in groups:
                n = len(pairs)
                for i, (lt, rh) in enumerate(pairs):
                    ins = e.matmul(out, lhsT=lt, rhs=rh, start=(sf and i == 0), stop=(sl and i == n - 1))
            return ins
        return fn

    def F_tr(items):
        def fn(e):
            ins = None
            for out, in_, idn in items:
                ins = e.transpose(out=out, in_=in_, identity=idn)
            return ins
        return fn

    def F_dma(items, s):
        def fn(e):
            ins = None
            for out, in_ in items:
                ins = e.dma_start(out=out, in_=in_)
                ins.then_inc(s, 16)
            return ins
        return fn

    J(st_gp, [], ["prm"], F_dma([(prmT[:, :], prm[:, :])], sem_misc), dma=(sem_misc, 1))
    J(st_dve, [], ["ones"], F_memset(onesm[:, :], 1.0 / 2048.0))
    J(st_dve, [], [("vp", kb) for kb in range(12)], F_memset(Vp[:, :, :, :], 1.0))
    J(st_dve, [], [A(b) for b in range(16)], F_memset(bufA[:, :], 0.0))
    J(st_dve, [], ["zt"], F_memset(zt[:, :], 0.0))
    J(st_dve, [], [("esb", h) for h in range(16)], F_memset(esrow[:, :], 0.0))
    J(st_dve, [], ["zo"], F_memset(zo[:, :], 0.0))
    J(st_dve, ["zo"], ["zo"], F_memset(zo[0:1, 64:128], 1.0))

    cc_items = []
    for l in range(2):
        cc_items.append((kso[l, :, 0:64, :], ck[l, :, 64:128, :]))
        cc_items.append((vso[l, :, 0:64, :], cv[l, :, 64:128, :]))

    cast_keys = {}

    def emit_casts():
        idx = 0
        for l in range(2):
            for n in ("in", "pw", "out", "gate", "up", "down"):
                src = w_src[n][l].rearrange("r c -> (r c)").rearrange("(a b) -> a b", b=2048)
                dst = w_bf[n][l].rearrange("r c -> (r c)").rearrange("(a b) -> a b", b=2048)
                nd = src.shape[0]
                step = {"in": 1280, "pw": 128, "out": 1024}.get(n, 1408)
                keys = []
                for a0 in range(0, nd, step):
                    key = ("Wbp", idx)
                    sm = sem_castp[idx % 2]
                    reads = [("Wbp", idx - 2)] if idx >= 2 else []
                    J(st_gp, reads, [key], F_dma([(dst[a0:a0 + step, :], src[a0:a0 + step, :])], sm), dma=(sm, 1))
                    keys.append(key)
                    idx += 1
                cast_keys[(l, n)] = keys

    def load_w(l, n, src_ap, dst_fn):
        s = next_slot()
        J(st_sp, cast_keys[(l, n)], [W(s)], F_dma([(dst_fn(s), src_ap)], sem_w[s]), dma=(sem_w[s], 1))
        return s

    class Rms:
        def __init__(self, T):
            self.T, self.b, self.n, self.pend = T, 7, 0, []

        def acc(self, sq_chunk_ap, key):
            c = self.n
            self.n += 1
            J(st_pe, [key, "ones"], [PS(self.b)],
              F_mm([(ps[:, self.b, 0:self.T], [(onesm[:, :], sq_chunk_ap)], c == 0, c == 15)]))

        def acc_delayed(self, sq_chunk_ap, key, delay=2):
            self.pend.append((sq_chunk_ap, key))
            if len(self.pend) > delay:
                self.acc(*self.pend.pop(0))

        def finish(self):
            for it in self.pend:
                self.acc(*it)
            self.pend = []
            T, b = self.T, self.b
            J(st_act, [PS(b)], ["rstd"], F_act(rstd[:, 0:T], ps[:, b, 0:T], AF.Ln, bias=EPS))
            J(st_act, ["rstd"], ["rstd"], F_act(rstd[:, 0:T], rstd[:, 0:T], AF.Exp, scale=-0.5))

    POOL_RES = (11, 12, 13, 14, 15)
    POOL_HN = (13, 14, 15)

    def hn_from_x(T, gcol, use_pool=False):
        for c in range(16):
            if use_pool and c in POOL_HN:
                J(st_gp, [X(c), "prm"], ["tmpP"],
                  lambda e, c=c: e.tensor_scalar(out=tmpP[:, 0:T], in0=xT[:, c, 0:T],
                                                 scalar1=prmT[:, gcol + c:gcol + c + 1], scalar2=None, op0=ALU.mult))
                J(st_gp, ["tmpP", "rstd"], [HN(c)], F_tt(hn[:, c, 0:T], tmpP[:, 0:T], rstd[:, 0:T], ALU.mult))
        for c in range(16):
            if use_pool and c in POOL_HN:
                continue
            J(st_dve, [X(c), "rstd", "prm"], [HN(c)],
              F_stt(hn[:, c, 0:T], xT[:, c, 0:T], prmT[:, gcol + c:gcol + c + 1], rstd[:, 0:T], ALU.mult, ALU.mult))

    def prenorm_big(T, gcol):
        J(st_act, XALL, [H(b) for b in range(28, 44)], F_act(sqH[:, :, 0:T], xT[:, :, 0:T], AF.Square))
        r = Rms(T)
        for c in range(16):
            r.acc(sqH[:, c, 0:T], H(28 + c))
        r.finish()
        hn_from_x(T, gcol)

    def residual_update(T, src, next_gcol, use_pool):
        r = Rms(T) if next_gcol is not None else None
        for c in range(16):
            st_e = st_gp if (use_pool and c in POOL_RES) else st_dve
            J(st_e, [A(c), "rstd"], [A(c)], F_tt(src[:, c, 0:T], src[:, c, 0:T], rstd[:, 0:T], ALU.mult))
            J(st_e, [A(c), X(c)], [X(c)], F_tt(xT[:, c, 0:T], xT[:, c, 0:T], src[:, c, 0:T], ALU.add))
            if r is not None:
                J(st_act, [X(c)], [H(28 + c)], F_act(sqH[:, c, 0:T], xT[:, c, 0:T], AF.Square))
                r.acc(sqH[:, c, 0:T], H(28 + c))
        if r is not None:
            r.finish()
            hn_from_x(T, next_gcol, use_pool)

    def wpanel_src(l, n, k0, nk, c0):
        return w_bf[n][l].rearrange("(kc p) c -> p kc c", p=128)[:, k0:k0 + nk, c0:c0 + 512]

    def tile_layer(tile, l, after_xload=None, use_pool=False):
        kind, T = tile["kind"], tile["T"]
        NCH = T // 64
        NTB = T // 128
        tno = tile.get("t", 0)
        is_p = kind == "P"
        last_p = is_p and tno == 3

        def seg(ap2d_T):
            return ap2d_T.rearrange("p (s t) -> p s t", s=4)

        if l == 0:
            for tb in range(NTB):
                src = xp[tno * 512 + tb * 128: tno * 512 + (tb + 1) * 128, :] if is_p else xs[tb * 128:(tb + 1) * 128, :]
                skeys = xstg_keys[tb]
                J(st_sp, [], skeys, F_dma([(xstg[tb], src)], sem_xl[tb]), dma=(sem_xl[tb], 1))
            for tb in range(NTB):
                skeys = xstg_keys[tb]
                for q in range(4):
                    b = next_bank()
                    J(st_pe, skeys + ["prm"], [PS(b)],
                      F_tr([(ps[:, b, j * 128:(j + 1) * 128], xstg[tb][:, (4 * q + j) * 128:(4 * q + j + 1) * 128], ident)
                            for j in range(4)]))
                    J(st_act, [PS(b)], [X(4 * q + j) for j in range(4)],
                      F_act(xT[:, 4 * q:4 * q + 4, tb * 128:(tb + 1) * 128],
                            ps[:, b, :].rearrange("p (j t) -> p j t", j=4), AF.Copy))

        if after_xload is not None:
            after_xload()

        J(st_act, ["prm"], ["es16"], F_act(es16[:, :], prmT[:, PC_SK + 16 * l:PC_SK + 16 * l + 16], AF.Exp))
        for h in range(16):
            J(st_dve, ["es16", "zt"], [("esb", h)],
              lambda e, h=h: e.tensor_scalar(out=esrow[0:1, h * 64:(h + 1) * 64], in0=zt[0:1, :],
                                             scalar1=es16[0:1, h:h + 1], scalar2=None, op0=ALU.add))

        if is_p:
            if tno == 0:
                J(st_dve, [], UKEYS, F_memset(U[:, :, 0:16], 0.0))
            else:
                J(st_act, [("kth", l)], [("kt", 0), ("kt", 1)], F_act(KT[:, :, 0:128], KTh[l][:, :, :], AF.Copy))
                J(st_act, [("vh", l)], [("vp", 0), ("vp", 1)], F_act(Vp[:, 0:2, :, :], Vh[l][:, :, :, :], AF.Copy))
                J(st_act, [("uh", l)], UKEYS, F_act(U[:, :, 0:16], Uh[l][:, :, :], AF.Copy))
        else:
            J(st_gp, [], AK(2560, 3584), F_dma([(ckst, ck[l].rearrange("s t f -> t s f"))], sem_ck), dma=(sem_ck, 1))
            for gs in range(2):
                b = next_bank()
                J(st_pe, AK(2560, 3584) + ["prm"], [PS(b)],
                  F_tr([(ps[:, b, s_ * 128:(s_ + 1) * 128], ckst[:, s_, gs * 128:(gs + 1) * 128], ident) for s_ in range(4)]))
                J(st_act, [PS(b)], [("kt", gs)],
                  F_act(KT[:, gs, :].rearrange("p (s e) -> p s e", s=4)[:, :, 0:128],
                        ps[:, b, :].rearrange("p (s t) -> p s t", s=4), AF.Copy))
            items = []
            for s_ in range(4):
                for k2 in range(2):
                    items.append((Vp[0:64, 3 * s_ + k2, :, 0:64],
                                  cv[l, s_, k2 * 64:(k2 + 1) * 64, :].rearrange("p (g d) -> p g d", g=4)))
            J(st_gp, [], [("vp", 3 * s_ + k2) for s_ in range(4) for k2 in range(2)], F_dma(items, sem_cv),
              dma=(sem_cv, len(items)))
            J(st_gp, [], AK(7168, 8192), F_dma([(spst, sp[l, :, :])], sem_sp), dma=(sem_sp, 1))
            b = next_bank()
            J(st_pe, AK(7168, 8192) + ["prm"], [PS(b)],
              F_tr([(ps[:, b, c * 60:(c + 1) * 60], spst[:, c * 128:(c + 1) * 128], prmT[0:60, PC_ID:PC_ID + 60])
                    for c in range(8)]))
            for c in range(8):
                J(st_act, [PS(b)], AK(c * 528, (c + 1) * 528),
                  F_act(U[:, c, 0:320].rearrange("p (s e) -> p s e", s=4)[:, :, 1:16],
                        ps[:, b, c * 60:(c + 1) * 60].rearrange("p (s r) -> p s r", s=4), AF.Copy))

        if l == 0:
            prenorm_big(T, PC_GMP + 16 * l)

        def u_out(c):
            if is_p:
                return U[:, c, 16:16 + T]
            return U[:, c, 0:320].rearrange("p (s e) -> p s e", s=4)[:, :, 16:80]

        def psT(b):
            return ps[:, b, 0:T] if is_p else seg(ps[:, b, 0:T])

        def pooling():
            for g, w in enumerate(POOL_WINS):
                if is_p:
                    views = [(U[:, 2 * g:2 * g + 2, :], tmpA.rearrange("p (c e) -> p c e", c=2),
                              tmpB.rearrange("p (c e) -> p c e", c=2), dT[:, 2 * g:2 * g + 2, 0:T], 528,
                              AK(2 * g * 528, (2 * g + 2) * 528), [H(16 + 2 * g), H(17 + 2 * g)], 2 * g)]
                else:
                    views = []
                    for c in (2 * g, 2 * g + 1):
                        views.append((U[:, c, 0:320].rearrange("p (s e) -> p s e", s=4),
                                      tmpA[:, 0:320].rearrange("p (s e) -> p s e", s=4),
                                      tmpB[:, 0:320].rearrange("p (s e) -> p s e", s=4),
                                      seg(dT[:, c, 0:T]), 80, AK(c * 528, (c + 1) * 528), [H(16 + c)], c))
                for V_, tA, tB, dO, E, ukeys, dkeys, c0 in views:
                    J(st_dve, ukeys, ["tA"], F_tt(tA[:, :, 1:E], V_[:, :, 1:E], V_[:, :, 0:E - 1], ALU.add))
                    fin, oth, fk, ok = tA, tB, "tA", "tB"
                    if w >= 4:
                        J(st_dve, ["tA"], ["tB"], F_tt(tB[:, :, 3:E], tA[:, :, 3:E], tA[:, :, 1:E - 2], ALU.add))
                        fin, oth, fk, ok = tB, tA, "tB", "tA"
                    if w >= 8:
                        J(st_dve, ["tB"], ["tA"], F_tt(tA[:, :, 7:E], tB[:, :, 7:E], tB[:, :, 3:E - 4], ALU.add))
                        fin, oth, fk, ok = tA, tB, "tA", "tB"
                    if w >= 16:
                        J(st_dve, ["tA"], ["tB"], F_tt(tB[:, :, 15:E], tA[:, :, 15:E], tA[:, :, 7:E - 8], ALU.add))
                        fin, oth, fk, ok = tB, tA, "tB", "tA"
                    J(st_dve, [fk] + ukeys, dkeys,
                      F_stt(dO, fin[:, :, 16:E], 1.0 / w, V_[:, :, 16:E], ALU.mult, ALU.subtract))
                    if is_p and tno == 0:
                        rc = prmT[:, PC_RC + c0 * 16:PC_RC + (c0 + 2) * 16].rearrange("p (c t) -> p c t", c=2)
                        J(st_dve, [fk, "prm"], [ok], F_tt(oth[:, :, 0:16], fin[:, :, 16:32], rc, ALU.mult))
                        J(st_dve, [ok] + ukeys, dkeys, F_tt(dO[:, :, 0:16], oth[:, :, 0:16], V_[:, :, 16:32], ALU.subtract))
            if is_p and tno < 3:
                J(st_act, UKEYS, [("uh", l)], F_act(Uh[l][:, :, :], U[:, :, 512:528], AF.Copy))


        u_slots = []
        for panel in range(5):
            s = load_w(l, "in", wpanel_src(l, "in", 0, 16, panel * 512), lambda s_: wsl[s_][:, :, :])
            if panel < 2:
                u_slots.append(s)
            nchunk = 4 if panel < 4 else 2
            for m in range(nchunk):
                b = next_bank()
                J(st_pe, [W(s)] + HNALL, [PS(b)],
                  F_mm([(ps[:, b, 0:T], [(wsl[s][:, kc, m * 128:(m + 1) * 128], hn[:, kc, 0:T]) for kc in range(16)],
                         True, True)]))
                if panel < 2:
                    c = panel * 4 + m
                    J(st_act, [PS(b)], AK(c * 528, (c + 1) * 528), F_act(u_out(c), psT(b), AF.Copy))
                elif panel < 4:
                    i = (panel - 2) * 4 + m
                    J(st_act, [PS(b)], [H(24 + i)], F_act(QT[:, i, 0:T], ps[:, b, 0:T], AF.Copy, scale=0.125))
                else:
                    gs = m
                    if is_p:
                        ko = KT[:, gs, 128:128 + T]
                    else:
                        ko = KT[:, gs, :].rearrange("p (s e) -> p s e", s=4)[:, :, 128:192]
                    J(st_act, [PS(b)], [("kt", gs)], F_act(ko, psT(b), AF.Copy))
            if panel == 1 and (last_p or not is_p):
                chunks = [7] if is_p else [0, 1, 2, 3]
                for cq in chunks:
                    for pn in range(2):
                        b = next_bank()
                        sl = u_slots[pn]
                        J(st_pe, [W(sl)] + HNALL, [PS(b)],
                          F_mm([(ps[0:64, b, 0:512],
                                 [(hn[:, kc, cq * 64:(cq + 1) * 64], wsl[sl][:, kc, :]) for kc in range(16)], True, True)]))
                        J(st_dve, [PS(b)], [A(14 + pn)], F_copy(ust[:, pn * 512:(pn + 1) * 512], ps[0:64, b, 0:512]))
                    dst = ppo[l, :, :] if is_p else pso[l, cq, :, :]
                    J(st_gp, [A(14), A(15)], [], F_dma([(dst, bufA[49:64, 7168:8192])], sem_u), dma=(sem_u, 1))
            if panel == 1:
                pooling()
            if panel == 4:
                for cq in range(NCH):
                    needK = (last_p and cq >= 6) or (not is_p)
                    kb_own = (2 + cq) if is_p else (3 * cq + 2)
                    b = next_bank()
                    c0 = 0 if needK else 256
                    N = 512 - c0
                    J(st_pe, [W(s)] + HNALL, [PS(b)],
                      F_mm([(ps[0:64, b, 0:N],
                             [(hn[:, kc, cq * 64:(cq + 1) * 64], wsl[s][:, kc, c0:512]) for kc in range(16)], True, True)]))
                    J(st_dve, [PS(b)], [("vp", kb_own)],
                      F_copy(Vp[0:64, kb_own, :, 0:64], ps[0:64, b, N - 256:N].rearrange("p (g d) -> p g d", g=4)))
                    if needK:
                        J(st_act, [PS(b)], [A(13)], F_act(kvst[:, :], ps[0:64, b, 0:512], AF.Copy))
                        if is_p:
                            r0 = (cq - 6) * 64
                            items = [(kpo[l, r0:r0 + 64, :], kvst[:, 0:256]), (vpo[l, r0:r0 + 64, :], kvst[:, 256:512])]
                        else:
                            items = [(kso[l, cq, 64:128, :], kvst[:, 0:256]), (vso[l, cq, 64:128, :], kvst[:, 256:512])]
                        J(st_gp, [A(13)], [], F_dma(items, sem_kv), dma=(sem_kv, 2))

        s = load_w(l, "pw", w_bf["pw"][l].rearrange("(j p) c -> p j c", p=128),
                   lambda s_: wsl[s_][:, 0:4, :].rearrange("p a (b c) -> p (a b) c", c=256))
        pwv = wsl[s][:, 0:4, :].rearrange("p a (b c) -> p (a b) c", c=256)
        for g in range(4):
            for mo in range(2):
                b = next_bank()
                J(st_pe, [W(s), H(16 + 2 * g), H(17 + 2 * g)], [PS(b)],
                  F_mm([(ps[:, b, 0:T], [(pwv[:, 2 * g + ki, mo * 128:(mo + 1) * 128], dT[:, 2 * g + ki, 0:T])
                                         for ki in range(2)], True, True)]))
                cc = 2 * g + mo
                J(st_act, [PS(b), "prm"], [H(cc)],
                  F_act(catT[:, cc, 0:T], ps[:, b, 0:T], AF.Copy, scale=prmT[:, PC_PS + 8 * l + cc:PC_PS + 8 * l + cc + 1]))

        jobs = []
        for c in range(NCH):
            if is_p:
                kbs = [kb for kb in (c, c + 1, c + 2) if not (tno == 0 and kb < 2)]
            else:
                kbs = [3 * c, 3 * c + 1, 3 * c + 2]
            for gs in range(2):
                for hf in range(2):
                    jobs.append((c, gs, hf, kbs))

        def qk(j):
            c, gs, hf, kbs = jobs[j]
            nk = len(kbs)
            base = 2 * (j % 3)
            groups = []
            for ki, kb in enumerate(kbs):
                o0 = base * 512 + ki * 256
                groups.append((psf[0:64, o0:o0 + 256],
                               [(KT[hf * 64:(hf + 1) * 64, gs, kb * 64:(kb + 1) * 64],
                                 QT[hf * 64:(hf + 1) * 64, 4 * gs:4 * gs + 4, c * 64:(c + 1) * 64])], True, True))
            J(st_pe, [("kt", gs)] + [H(24 + 4 * gs + i) for i in range(4)], [PS(base), PS(base + 1)], F_mm(groups))
            n = nk * 256
            J(st_act, [PS(base), PS(base + 1)], [("pt", j % 3)],
              F_act(PT[j % 3][:, 0:n], psf[0:64, base * 512:base * 512 + n], AF.Exp))

        def pv(j):
            c, gs, hf, kbs = jobs[j]
            nk = len(kbs)
            bo = 6 + (j % 2)
            g = 2 * gs + hf
            e0 = (2 * gs + hf) * 256
            groups = [(ps[:, bo, 0:256],
                       [(Vp[0:64, kb, g, :], PT[j % 3][:, ki * 256:(ki + 1) * 256]) for ki, kb in enumerate(kbs)]
                       + [(zo[:, :], esrow[:, e0:e0 + 256])],
                       True, True)]
            J(st_pe, [("pt", j % 3), "zo"] + [("vp", kb) for kb in kbs]
              + [("esb", h) for h in range(8 * gs + 4 * hf, 8 * gs + 4 * hf + 4)], [PS(bo)], F_mm(groups))
            ta, tb_ = t1[j % 2], t2[j % 2]
            J(st_act, [PS(bo)], ["t1"], F_act(ta[64:128, 0:256], ps[64:128, bo, 0:256], AF.Ln))
            J(st_act, ["t1"], [("t2", j % 2)], F_act(tb_[0:64, 0:256], ta[64:128, 0:256], AF.Exp, scale=-1.0))
            J(st_dve, [PS(bo), ("t2", j % 2)], [H(8 + 4 * gs + i) for i in range(4)],
              F_tt(catT[hf * 64:(hf + 1) * 64, 8 + 4 * gs:12 + 4 * gs, c * 64:(c + 1) * 64],
                   ps[0:64, bo, 0:256].rearrange("p (i q) -> p i q", i=4),
                   tb_[0:64, 0:256].rearrange("p (i q) -> p i q", i=4), ALU.mult))

        nj = len(jobs)
        qk(0)
        if nj > 1:
            qk(1)
        for j in range(nj):
            if j + 2 < nj:
                qk(j + 2)
            pv(j)
        if is_p and tno < 3:
            J(st_act, [("kt", 0), ("kt", 1)], [("kth", l)], F_act(KTh[l][:, :, :], KT[:, :, 512:640], AF.Copy))
            J(st_act, [("vp", 8), ("vp", 9)], [("vh", l)], F_act(Vh[l][:, :, :, :], Vp[:, 8:10, :, :], AF.Copy))

        gcol = PC_GMO + 16 * l
        rm = Rms(T)
        for panel in range(4):
            s = load_w(l, "out", wpanel_src(l, "out", 0, 16, panel * 512), lambda s_: wsl[s_][:, :, :])
            for m in range(4):
                co = 4 * panel + m
                b = next_bank()
                J(st_pe, [W(s)] + [H(k) for k in range(16)], [PS(b)],
                  F_mm([(ps[:, b, 0:T], [(wsl[s][:, kc, m * 128:(m + 1) * 128], catT[:, kc, 0:T]) for kc in range(16)],
                         True, True)]))
                J(st_act, [PS(b)], [H(28 + co)], F_act(sqH[:, co, 0:T], ps[:, b, 0:T], AF.Square))
                J(st_act, [PS(b), "prm"], [A(co)],
                  F_act(mix[:, co, 0:T], ps[:, b, 0:T], AF.Copy, scale=prmT[:, gcol + co:gcol + co + 1]))
                rm.acc_delayed(sqH[:, co, 0:T], H(28 + co))
        rm.finish()
        residual_update(T, mix, PC_GFP + 16 * l, use_pool)

        for grp in range(11):
            s_g = load_w(l, "gate", wpanel_src(l, "gate", 0, 16, grp * 512), lambda s_: wsl[s_][:, :, :])
            s_u = load_w(l, "up", wpanel_src(l, "up", 0, 16, grp * 512), lambda s_: wsl[s_][:, :, :])
            for m in range(4):
                b = next_bank()
                J(st_pe, [W(s_g)] + HNALL, [PS(b)],
                  F_mm([(ps[:, b, 0:T], [(wsl[s_g][:, kc, m * 128:(m + 1) * 128], hn[:, kc, 0:T]) for kc in range(16)],
                         True, True)]))
                J(st_act, [PS(b)], [("sg", m)], F_act(sg[m][:, 0:T], ps[:, b, 0:T], AF.Silu))
            for m in range(4):
                k = 4 * grp + m
                b = next_bank()
                J(st_pe, [W(s_u)] + HNALL, [PS(b)],
                  F_mm([(ps[:, b, 0:T], [(wsl[s_u][:, kc, m * 128:(m + 1) * 128], hn[:, kc, 0:T]) for kc in range(16)],
                         True, True)]))
                J(st_dve, [PS(b), ("sg", m)], [H(k)], F_tt(hid[:, k, 0:T], sg[m][:, 0:T], ps[:, b, 0:T], ALU.mult))
        gcol = PC_GFO + 16 * l
        rm = Rms(T)
        for panel in range(4):
            banks = [next_bank() for _ in range(4)]
            for (k0, nk) in ((0, 16), (16, 16), (32, 12)):
                s = load_w(l, "down", wpanel_src(l, "down", k0, nk, panel * 512),
                           lambda s_, nk=nk: wsl[s_][:, 0:nk, :])
                for m in range(4):
                    b = banks[m]
                    J(st_pe, [W(s)] + [H(k0 + jj) for jj in range(nk)], [PS(b)],
                      F_mm([(ps[:, b, 0:T],
                             [(wsl[s][:, jj, m * 128:(m + 1) * 128], hid[:, k0 + jj, 0:T]) for jj in range(nk)],
                             k0 == 0, k0 + nk == 44)]))
            for m in range(4):
                co = 4 * panel + m
                b = banks[m]
                J(st_act, [PS(b)], [HN(co)], F_act(hn[:, co, 0:T], ps[:, b, 0:T], AF.Square))
                J(st_act, [PS(b), "prm"], [A(co)],
                  F_act(mix[:, co, 0:T], ps[:, b, 0:T], AF.Copy, scale=prmT[:, gcol + co:gcol + co + 1]))
                rm.acc_delayed(hn[:, co, 0:T], HN(co))
        rm.finish()
        residual_update(T, mix, (PC_GMP + 16 * (l + 1)) if l + 1 < NL else None, use_pool)

        if l == NL - 1:
            for tb in range(NTB):
                k = tb % 2
                for q in range(4):
                    b = next_bank()
                    J(st_pe, [X(4 * q + j) for j in range(4)] + ["prm"], [PS(b)],
                      F_tr([(ps[:, b, j * 128:(j + 1) * 128], xT[:, 4 * q + j, tb * 128:(tb + 1) * 128], ident)
                            for j in range(4)]))
                    J(st_act, [PS(b)], [HN(8 * k + 2 * q), HN(8 * k + 2 * q + 1)],
                      F_act(stg[k][:, q * 512:(q + 1) * 512], ps[:, b, :], AF.Copy))
                dst = yp[tno * 512 + tb * 128: tno * 512 + (tb + 1) * 128, :] if is_p else ys[tb * 128:(tb + 1) * 128, :]
                J(st_gp, [HN(8 * k + i) for i in range(8)], [], F_dma([(dst, stg[k])], sem_ys[k]), dma=(sem_ys[k], 1))

    tiles = [dict(kind="P", t=t, T=512) for t in range(4)] + [dict(kind="S", T=256)]

    def prologue_dma():
        J(st_gp, [], [], F_dma(cc_items, sem_cc), dma=(sem_cc, len(cc_items)))
        emit_casts()

    if tile_sel is not None:
        tiles = [tiles[i] for i in tile_sel]
    for ti, tile in enumerate(tiles):
        for l in range(NL):
            tile_layer(tile, l, after_xload=prologue_dma if (ti == 0 and l == 0) else None, use_pool=False)

    finals = []
    for sm in (sem_ys[0], sem_ys[1], sem_kv, sem_u, sem_cc):
        v = cx.dma_cnt.get(id(sm), 0)
        if v:
            finals.append((sm, v))

    with nc.Block() as block:
        @block.tensor
        def _(e):
            _run(st_pe, e)

        @block.scalar
        def _(e):
            _run(st_act, e)

        @block.vector
        def _(e):
            _run(st_dve, e)

        @block.sync
        def _(e):
            _run(st_sp, e)

        @block.gpsimd
        def _(e):
            _run(st_gp, e)
            for sm, v in finals:
                e.wait_ge(sm, v)

    es.close()
    return nc


_NC_CACHE = {}


def _get_nc():
    if "nc" not in _NC_CACHE:
        _NC_CACHE["nc"] = build_nc()
    return _NC_CACHE["nc"]


def _prep_params(inp):
    prm = np.zeros((128, NPRM), np.float32)

    def fm(g):
        return np.ascontiguousarray(np.asarray(g, np.float32).reshape(16, 128).T)

    for l in range(NL):
        prm[:, PC_GMP + 16 * l:PC_GMP + 16 * l + 16] = fm(inp["g_mix_pre"][l])
        prm[:, PC_GMO + 16 * l:PC_GMO + 16 * l + 16] = fm(inp["g_mix_post"][l])
        prm[:, PC_GFP + 16 * l:PC_GFP + 16 * l + 16] = fm(inp["g_ffn_pre"][l])
        prm[:, PC_GFO + 16 * l:PC_GFO + 16 * l + 16] = fm(inp["g_ffn_post"][l])
        prm[:, PC_PS + 8 * l:PC_PS + 8 * l + 8] = np.asarray(inp["pool_scale"][l], np.float32).reshape(8, 128).T
        sk = np.asarray(inp["attn_sinks"][l], np.float32)
        order = [A_IDX[4 * gs + i] + 4 * hf for gs in range(2) for hf in range(2) for i in range(4)]
        prm[:, PC_SK + 16 * l:PC_SK + 16 * l + 16] = sk[order][None, :]
    rc = np.zeros((8, 16), np.float32)
    for c in range(8):
        w = POOL_WINS[c // 2]
        for t in range(16):
            rc[c, t] = 1.0 / min(t + 1, w)
    prm[:, PC_RC:PC_RC + 128] = rc.reshape(1, 128)
    prm[:, PC_ID:PC_ID + 128] = np.eye(128, dtype=np.float32)
    return prm


def kernel(**inputs):
    inp = {k: np.asarray(v) for k, v in inputs.items()}
    n = 8
    qorder = [A_IDX[i] + 4 * hf for i in range(8) for hf in range(2)]
    w_in = inp["w_in"].astype(np.float32, copy=False)
    qcols = np.concatenate([np.arange(1024 + h * 64, 1024 + (h + 1) * 64) for h in qorder])
    cols = np.concatenate([np.arange(0, 1024), qcols, np.arange(2048, 2560)])
    w_in_p = np.ascontiguousarray(w_in[:, :, cols])
    rows = np.concatenate([np.arange(0, 1024), qcols])
    w_out_p = np.ascontiguousarray(inp["w_out"].astype(np.float32, copy=False)[:, rows, :])
    pool_w = np.ascontiguousarray(inp["pool_w"].astype(np.float32, copy=False).reshape(2, 1024, 256))
    prm = _prep_params(inp)
    shared = {
        "prm": prm, "w_in": w_in_p, "pool_w": pool_w, "w_out": w_out_p,
        "w_gate": np.ascontiguousarray(inp["w_gate"], dtype=np.float32),
        "w_up": np.ascontiguousarray(inp["w_up"], dtype=np.float32),
        "w_down": np.ascontiguousarray(inp["w_down"], dtype=np.float32),
    }
    in_maps = []
    for c in range(n):
        m = dict(shared)
        m["xp"] = np.ascontiguousarray(inp["x_prompt"][c], dtype=np.float32)
        m["xs"] = np.ascontiguousarray(inp["x_sample"][4 * c:4 * c + 4].reshape(256, D), dtype=np.float32)
        m["ck"] = np.ascontiguousarray(inp["cache_k"][:, 4 * c:4 * c + 4].reshape(2, 4, 128, 256), dtype=np.float32)
        m["cv"] = np.ascontiguousarray(inp["cache_v"][:, 4 * c:4 * c + 4].reshape(2, 4, 128, 256), dtype=np.float32)
        m["sp"] = np.ascontiguousarray(inp["state_pool"][:, 4 * c:4 * c + 4].reshape(2, 60, 1024), dtype=np.float32)
        in_maps.append(m)
    nc = _get_nc()
    res = run_bass_kernel_spmd(nc, in_maps, core_ids=list(range(n)))
    R = res.results
    y_prompt = np.stack([np.asarray(R[c]["yp"], np.float32) for c in range(n)], 0)
    y_sample = np.concatenate([np.asarray(R[c]["ys"], np.float32).reshape(4, 64, D) for c in range(n)], 0)
    kp = np.stack([np.asarray(R[c]["kpo"], np.float32).reshape(2, 128, 4, 64) for c in range(n)], 1)
    vp = np.stack([np.asarray(R[c]["vpo"], np.float32).reshape(2, 128, 4, 64) for c in range(n)], 1)
    pp = np.stack([np.asarray(R[c]["ppo"], np.float32) for c in range(n)], 1)
    ks = np.concatenate([np.asarray(R[c]["kso"], np.float32).reshape(2, 4, 128, 4, 64) for c in range(n)], 1)
    vs = np.concatenate([np.asarray(R[c]["vso"], np.float32).reshape(2, 4, 128, 4, 64) for c in range(n)], 1)
    pss = np.concatenate([np.asarray(R[c]["pso"], np.float32) for c in range(n)], 1)
    return (y_prompt, y_sample, kp, vp, pp, ks, vs, pss)
```

```python
from contextlib import ExitStack

import numpy as np
import concourse.bass as bass
import concourse.mybir as mybir
from concourse.bass_utils import run_bass_kernel_spmd

F32 = mybir.dt.float32
BF16 = mybir.dt.bfloat16
AF = mybir.ActivationFunctionType
ALU = mybir.AluOpType

D = 2048
DFF = 5632
NL = 2
EPS = 1e-6
A_IDX = [0, 1, 2, 3, 8, 9, 10, 11]
POOL_WINS = (2, 4, 8, 16)

PC_GMP, PC_GMO, PC_GFP, PC_GFO = 0, 32, 64, 96
PC_PS = 128
PC_SK = 144
PC_RC = 176
PC_ID = 304
NPRM = 432


class Stream:
    def __init__(self, name, sem, selfsync):
        self.name, self.sem, self.selfsync = name, sem, selfsync
        self.cnt = 0
        self.ops = []
        self.waited = {}


class Ctx:
    def __init__(self):
        self.last_w = {}
        self.readers = {}
        self.dma_cnt = {}

    def job(self, st, reads, writes, fn, dma=None):
        writes = list(writes) + [k for k in reads if k[0] == "ps"]
        reads = [k for k in reads if k[0] != "ps"]
        toks = {}

        def add(tok):
            if tok is None:
                return
            k = id(tok[0])
            if k not in toks or toks[k][1] < tok[1]:
                toks[k] = tok

        for k in reads:
            add(self.last_w.get(k))
        for k in writes:
            add(self.last_w.get(k))
            for t in self.readers.get(k, {}).values():
                add(t)
        for s, v in toks.values():
            if s is st.sem and not st.selfsync:
                continue
            if st.waited.get(id(s), 0) >= v:
                continue
            st.waited[id(s)] = v
            st.ops.append(("w", s, v))
        if dma is None:
            st.cnt += 1
            tok = (st.sem, st.cnt)
            st.ops.append(("e", fn, True))
        else:
            sem, n = dma
            self.dma_cnt[id(sem)] = self.dma_cnt.get(id(sem), 0) + 16 * n
            tok = (sem, self.dma_cnt[id(sem)])
            st.ops.append(("e", fn, False))
        for k in reads:
            r = self.readers.setdefault(k, {})
            o = r.get(id(tok[0]))
            if o is None or o[1] < tok[1]:
                r[id(tok[0])] = tok
        for k in writes:
            self.last_w[k] = tok
            self.readers[k] = {}
        return tok


def _run(st, e):
    for op in st.ops:
        if op[0] == "w":
            e.wait_ge(op[1], op[2])
        else:
            ins = op[1](e)
            if op[2]:
                ins.then_inc(st.sem, 1)


def build_nc(tile_sel=None):
    nc = bass.Bass("TRN2", target_bir_lowering=False)

    def din(name, shape):
        return nc.dram_tensor(name, shape, F32, kind="ExternalInput").ap()

    def dout(name, shape):
        return nc.dram_tensor(name, shape, F32, kind="ExternalOutput").ap()

    def dscr(name, shape):
        return nc.dram_tensor(name, shape, BF16, kind="Internal").ap()

    xp = din("xp", [2048, D])
    xs = din("xs", [256, D])
    ck = din("ck", [2, 4, 128, 256])
    cv = din("cv", [2, 4, 128, 256])
    sp = din("sp", [2, 60, 1024])
    prm = din("prm", [128, NPRM])
    w_src = {
        "in": din("w_in", [2, D, 2560]),
        "pw": din("pool_w", [2, 1024, 256]),
        "out": din("w_out", [2, D, D]),
        "gate": din("w_gate", [2, D, DFF]),
        "up": din("w_up", [2, D, DFF]),
        "down": din("w_down", [2, DFF, D]),
    }
    w_bf = {
        "in": dscr("b_in", [2, D, 2560]),
        "pw": dscr("b_pw", [2, 1024, 256]),
        "out": dscr("b_out", [2, D, D]),
        "gate": dscr("b_gate", [2, D, DFF]),
        "up": dscr("b_up", [2, D, DFF]),
        "down": dscr("b_down", [2, DFF, D]),
    }
    yp = dout("yp", [2048, D])
    ys = dout("ys", [256, D])
    kpo = dout("kpo", [2, 128, 256])
    vpo = dout("vpo", [2, 128, 256])
    ppo = dout("ppo", [2, 15, 1024])
    kso = dout("kso", [2, 4, 128, 256])
    vso = dout("vso", [2, 4, 128, 256])
    pso = dout("pso", [2, 4, 15, 1024])

    es = ExitStack()

    def sb(name, shape, dt):
        return es.enter_context(nc.sbuf_tensor(name, shape, dt))

    def sem(name):
        return es.enter_context(nc.semaphore(name))

    xT = sb("xT", [128, 16, 512], F32)
    hn_raw = sb("hn", [128, 4096], F32)
    wsl = [sb(f"ws{i}", [128, 16, 512], BF16) for i in range(2)]
    bufA = sb("bufA", [128, 8192], F32)
    hid = sb("hid", [128, 44, 512], BF16)
    KT = sb("KT", [128, 2, 768], BF16)
    Vp = sb("Vp", [64, 12, 4, 128], BF16)
    PT = [sb(f"PT{i}", [64, 768], BF16) for i in range(3)]
    t1 = [sb("t1_0", [128, 256], F32)] * 2
    tmpP = sb("tmpP", [128, 512], F32)
    t2 = [sb(f"t2_{i}", [64, 256], F32) for i in range(2)]
    rstd = sb("rstd", [128, 512], F32)
    sg = [sb(f"sg{i}", [128, 512], F32) for i in range(4)]
    prmT = sb("prmT", [128, NPRM], F32)
    onesm = sb("onesm", [128, 128], BF16)
    esrow = sb("esrow", [64, 1024], BF16)
    zo = sb("zo", [64, 128], BF16)
    es16 = sb("es16", [128, 16], F32)
    zt = sb("zt", [128, 64], F32)
    KTh = [sb(f"KTh{l}", [128, 2, 128], BF16) for l in range(2)]
    Vh = [sb(f"Vh{l}", [64, 2, 4, 128], BF16) for l in range(2)]
    Uh = [sb(f"Uh{l}", [128, 8, 16], F32) for l in range(2)]
    ps = es.enter_context(nc.psum_tensor("ps", [128, 8, 512], F32))
    psf = ps[:, :, :].rearrange("p b t -> p (b t)")

    hn = hn_raw[:, :].bitcast(BF16).rearrange("p (c t) -> p c t", c=16)
    stg = [hn_raw[:, 0:2048], hn_raw[:, 2048:4096]]
    hid_f = hid[:, :, :].rearrange("p c t -> p (c t)").bitcast(F32)
    xstg = [hid_f[:, 0:2048], hid_f[:, 2048:4096], hid_f[:, 4096:6144], bufA[:, 0:2048]]
    xstg_keys = [[("H", b) for b in range(0, 8)], [("H", b) for b in range(8, 16)], [("H", b) for b in range(16, 24)],
                 [("A", b) for b in range(0, 4)]]
    U = bufA[:, 0:4224].rearrange("p (c e) -> p c e", c=8)
    tmpA = bufA[:, 4224:5280]
    tmpB = bufA[:, 5280:6336]
    kvst = bufA[0:64, 6656:7168]
    ust = bufA[0:64, 7168:8192]
    spst = bufA[0:60, 7168:8192]
    ckst = bufA[:, 2560:3584].rearrange("p (s f) -> p s f", s=4)
    mix = bufA[:, :].rearrange("p (c t) -> p c t", c=16)
    catT = hid[:, 0:16, :]
    dT = hid[:, 16:24, :]
    QT = hid[:, 24:32, :]
    sqH = hid[:, 28:44, :]
    ident = prmT[:, PC_ID:PC_ID + 128]

    st_pe = Stream("pe", sem("s_pe"), False)
    st_act = Stream("act", sem("s_act"), True)
    st_dve = Stream("dve", sem("s_dve"), True)
    st_gp = Stream("gp", sem("s_gp"), True)
    st_sp = Stream("sp", sem("s_sp"), False)
    sem_w = [sem("s_w0"), sem("s_w1")]
    sem_xl = [sem(f"s_xl{i}") for i in range(4)]
    sem_ys = [sem("s_ys0"), sem("s_ys1")]
    sem_misc = sem("s_misc")
    sem_ck = sem("s_ck")
    sem_cv = sem("s_cv")
    sem_sp = sem("s_spl")
    sem_kv = sem("s_kv")
    sem_u = sem("s_u")
    sem_cc = sem("s_cc")
    sem_cast = {(l, n): sem(f"s_c{l}{n}") for l in range(2) for n in w_src}

    cx = Ctx()
    J = cx.job

    def X(c):
        return ("x", c)

    def HN(c):
        return ("hn", c)

    def H(b):
        return ("H", b)

    def A(b):
        return ("A", b)

    def AK(lo, hi):
        return [A(b) for b in range(lo // 512, (hi - 1) // 512 + 1)]

    def PS(b):
        return ("ps", b)

    def W(s):
        return ("W", s)

    XALL = [X(c) for c in range(16)]
    HNALL = [HN(c) for c in range(16)]
    UKEYS = AK(0, 4224)

    bank_ctr = [0]

    def next_bank():
        b = bank_ctr[0] % 7
        bank_ctr[0] += 1
        return b

    slot_ctr = [0]

    def next_slot():
        s = slot_ctr[0] % 2
        slot_ctr[0] += 1
        return s

    def F_act(out, in_, func, **kw):
        return lambda e: e.activation(out=out, in_=in_, func=func, **kw)

    def F_tt(out, in0, in1, op):
        return lambda e: e.tensor_tensor(out=out, in0=in0, in1=in1, op=op)

    def F_stt(out, in0, scalar, in1, op0, op1):
        return lambda e: e.scalar_tensor_tensor(out=out, in0=in0, scalar=scalar, in1=in1, op0=op0, op1=op1)

    def F_copy(out, in_):
        return lambda e: e.tensor_copy(out=out, in_=in_)

    def F_memset(out, v):
        return lambda e: e.memset(out, v)

    def F_recip(out, in_):
        return lambda e: e.reciprocal(out=out, in_=in_)

    def F_mm(groups):
        def fn(e):
            ins = None
            for out, pairs, sf, sl in groups:
                n = len(pairs)
                for i, (lt, rh) in enumerate(pairs):
                    ins = e.matmul(out, lhsT=lt, rhs=rh, start=(sf and i == 0), stop=(sl and i == n - 1))
            return ins
        return fn

    def F_tr(items):
        def fn(e):
            ins = None
            for out, in_, idn in items:
                ins = e.transpose(out=out, in_=in_, identity=idn)
            return ins
        return fn

    def F_dma(items, s):
        def fn(e):
            ins = None
            for out, in_ in items:
                ins = e.dma_start(out=out, in_=in_)
                ins.then_inc(s, 16)
            return ins
        return fn

    J(st_gp, [], ["prm"], F_dma([(prmT[:, :], prm[:, :])], sem_misc), dma=(sem_misc, 1))
    J(st_dve, [], ["ones"], F_memset(onesm[:, :], 1.0 / 2048.0))
    J(st_dve, [], [("vp", kb) for kb in range(12)], F_memset(Vp[:, :, :, :], 1.0))
    J(st_dve, [], [A(b) for b in range(16)], F_memset(bufA[:, :], 0.0))
    J(st_dve, [], ["zt"], F_memset(zt[:, :], 0.0))
    J(st_dve, [], [("esb", h) for h in range(16)], F_memset(esrow[:, :], 0.0))
    J(st_dve, [], ["zo"], F_memset(zo[:, :], 0.0))
    J(st_dve, ["zo"], ["zo"], F_memset(zo[0:1, 64:128], 1.0))

    cc_items = []
    for l in range(2):
        cc_items.append((kso[l, :, 0:64, :], ck[l, :, 64:128, :]))
        cc_items.append((vso[l, :, 0:64, :], cv[l, :, 64:128, :]))

    def emit_casts():
        order = []
        for l in range(2):
            for n in ("in", "pw", "out", "gate", "up", "down"):
                order.append((l, n))
        for idx, (l, n) in enumerate(order):
            src, dst = w_src[n], w_bf[n]
            rows, cols = src.shape[1], src.shape[2]
            items = []
            csplit = 1
            while cols // csplit > 2048 or cols % csplit:
                csplit += 1
            cw = cols // csplit
            for r0 in range(0, rows, 512):
                r1 = min(rows, r0 + 512)
                for ci in range(csplit):
                    items.append((dst[l, r0:r1, ci * cw:(ci + 1) * cw], src[l, r0:r1, ci * cw:(ci + 1) * cw]))
            reads = []
            if idx >= 1:
                reads.append(("Wb",) + order[idx - 1])

            def fn(e, items=items, sm=sem_cast[(l, n)]):
                ins = None
                for i, (o, i_) in enumerate(items):
                    if i >= 2 and i % 2 == 0:
                        e.wait_ge(sm, 16 * i)
                    ins = e.dma_start(out=o, in_=i_)
                    ins.then_inc(sm, 16)
                return ins
            J(st_gp, reads, [("Wb", l, n)], fn, dma=(sem_cast[(l, n)], len(items)))

    def load_w(l, n, src_ap, dst_fn):
        s = next_slot()
        J(st_sp, [("Wb", l, n)], [W(s)], F_dma([(dst_fn(s), src_ap)], sem_w[s]), dma=(sem_w[s], 1))
        return s

    class Rms:
        def __init__(self, T):
            self.T, self.b, self.n, self.pend = T, 7, 0, []

        def acc(self, sq_chunk_ap, key):
            c = self.n
            self.n += 1
            J(st_pe, [key, "ones"], [PS(self.b)],
              F_mm([(ps[:, self.b, 0:self.T], [(onesm[:, :], sq_chunk_ap)], c == 0, c == 15)]))

        def acc_delayed(self, sq_chunk_ap, key, delay=2):
            self.pend.append((sq_chunk_ap, key))
            if len(self.pend) > delay:
                self.acc(*self.pend.pop(0))

        def finish(self):
            for it in self.pend:
                self.acc(*it)
            self.pend = []
            T, b = self.T, self.b
            J(st_act, [PS(b)], ["rstd"], F_act(rstd[:, 0:T], ps[:, b, 0:T], AF.Ln, bias=EPS))
            J(st_act, ["rstd"], ["rstd"], F_act(rstd[:, 0:T], rstd[:, 0:T], AF.Exp, scale=-0.5))

    POOL_RES = (11, 12, 13, 14, 15)
    POOL_HN = (13, 14, 15)

    def hn_from_x(T, gcol, use_pool=False):
        for c in range(16):
            if use_pool and c in POOL_HN:
                J(st_gp, [X(c), "prm"], ["tmpP"],
                  lambda e, c=c: e.tensor_scalar(out=tmpP[:, 0:T], in0=xT[:, c, 0:T],
                                                 scalar1=prmT[:, gcol + c:gcol + c + 1], scalar2=None, op0=ALU.mult))
                J(st_gp, ["tmpP", "rstd"], [HN(c)], F_tt(hn[:, c, 0:T], tmpP[:, 0:T], rstd[:, 0:T], ALU.mult))
        for c in range(16):
            if use_pool and c in POOL_HN:
                continue
            J(st_dve, [X(c), "rstd", "prm"], [HN(c)],
              F_stt(hn[:, c, 0:T], xT[:, c, 0:T], prmT[:, gcol + c:gcol + c + 1], rstd[:, 0:T], ALU.mult, ALU.mult))

    def prenorm_big(T, gcol):
        J(st_act, XALL, [H(b) for b in range(28, 44)], F_act(sqH[:, :, 0:T], xT[:, :, 0:T], AF.Square))
        r = Rms(T)
        for c in range(16):
            r.acc(sqH[:, c, 0:T], H(28 + c))
        r.finish()
        hn_from_x(T, gcol)

    def residual_update(T, src, next_gcol, use_pool):
        r = Rms(T) if next_gcol is not None else None
        for c in range(16):
            st_e = st_gp if (use_pool and c in POOL_RES) else st_dve
            J(st_e, [A(c), "rstd"], [A(c)], F_tt(src[:, c, 0:T], src[:, c, 0:T], rstd[:, 0:T], ALU.mult))
            J(st_e, [A(c), X(c)], [X(c)], F_tt(xT[:, c, 0:T], xT[:, c, 0:T], src[:, c, 0:T], ALU.add))
            if r is not None:
                J(st_act, [X(c)], [H(28 + c)], F_act(sqH[:, c, 0:T], xT[:, c, 0:T], AF.Square))
                r.acc(sqH[:, c, 0:T], H(28 + c))
        if r is not None:
            r.finish()
            hn_from_x(T, next_gcol, use_pool)

    def mm_trailing(out_ap, slot_key, pairs, bank):
        n = len(pairs)
        for kc, (lt, rh) in enumerate(pairs):
            J(st_pe, [slot_key, HN(kc)], [PS(bank)], F_mm([(out_ap, [(lt, rh)], kc == 0, kc == n - 1)]))

    def wpanel_src(l, n, k0, nk, c0):
        return w_bf[n][l].rearrange("(kc p) c -> p kc c", p=128)[:, k0:k0 + nk, c0:c0 + 512]

    def tile_layer(tile, l, after_xload=None, use_pool=False):
        kind, T = tile["kind"], tile["T"]
        NCH = T // 64
        NTB = T // 128
        tno = tile.get("t", 0)
        is_p = kind == "P"
        last_p = is_p and tno == 3

        def seg(ap2d_T):
            return ap2d_T.rearrange("p (s t) -> p s t", s=4)

        if l == 0:
            for tb in range(NTB):
                src = xp[tno * 512 + tb * 128: tno * 512 + (tb + 1) * 128, :] if is_p else xs[tb * 128:(tb + 1) * 128, :]
                skeys = xstg_keys[tb]
                J(st_sp, [], skeys, F_dma([(xstg[tb], src)], sem_xl[tb]), dma=(sem_xl[tb], 1))
            for tb in range(NTB):
                skeys = xstg_keys[tb]
                for q in range(4):
                    b = next_bank()
                    J(st_pe, skeys + ["prm"], [PS(b)],
                      F_tr([(ps[:, b, j * 128:(j + 1) * 128], xstg[tb][:, (4 * q + j) * 128:(4 * q + j + 1) * 128], ident)
                            for j in range(4)]))
                    J(st_act, [PS(b)], [X(4 * q + j) for j in range(4)],
                      F_act(xT[:, 4 * q:4 * q + 4, tb * 128:(tb + 1) * 128],
                            ps[:, b, :].rearrange("p (j t) -> p j t", j=4), AF.Copy))

        if after_xload is not None:
            after_xload()

        J(st_act, ["prm"], ["es16"], F_act(es16[:, :], prmT[:, PC_SK + 16 * l:PC_SK + 16 * l + 16], AF.Exp))
        for h in range(16):
            J(st_dve, ["es16", "zt"], [("esb", h)],
              lambda e, h=h: e.tensor_scalar(out=esrow[0:1, h * 64:(h + 1) * 64], in0=zt[0:1, :],
                                             scalar1=es16[0:1, h:h + 1], scalar2=None, op0=ALU.add))

        if is_p:
            if tno == 0:
                J(st_dve, [], UKEYS, F_memset(U[:, :, 0:16], 0.0))
            else:
                J(st_act, [("kth", l)], [("kt", 0), ("kt", 1)], F_act(KT[:, :, 0:128], KTh[l][:, :, :], AF.Copy))
                J(st_act, [("vh", l)], [("vp", 0), ("vp", 1)], F_act(Vp[:, 0:2, :, :], Vh[l][:, :, :, :], AF.Copy))
                J(st_act, [("uh", l)], UKEYS, F_act(U[:, :, 0:16], Uh[l][:, :, :], AF.Copy))
        else:
            J(st_gp, [], AK(2560, 3584), F_dma([(ckst, ck[l].rearrange("s t f -> t s f"))], sem_ck), dma=(sem_ck, 1))
            for gs in range(2):
                b = next_bank()
                J(st_pe, AK(2560, 3584) + ["prm"], [PS(b)],
                  F_tr([(ps[:, b, s_ * 128:(s_ + 1) * 128], ckst[:, s_, gs * 128:(gs + 1) * 128], ident) for s_ in range(4)]))
                J(st_act, [PS(b)], [("kt", gs)],
                  F_act(KT[:, gs, :].rearrange("p (s e) -> p s e", s=4)[:, :, 0:128],
                        ps[:, b, :].rearrange("p (s t) -> p s t", s=4), AF.Copy))
            items = []
            for s_ in range(4):
                for k2 in range(2):
                    items.append((Vp[0:64, 3 * s_ + k2, :, 0:64],
                                  cv[l, s_, k2 * 64:(k2 + 1) * 64, :].rearrange("p (g d) -> p g d", g=4)))
            J(st_gp, [], [("vp", 3 * s_ + k2) for s_ in range(4) for k2 in range(2)], F_dma(items, sem_cv),
              dma=(sem_cv, len(items)))
            J(st_gp, [], AK(7168, 8192), F_dma([(spst, sp[l, :, :])], sem_sp), dma=(sem_sp, 1))
            b = next_bank()
            J(st_pe, AK(7168, 8192) + ["prm"], [PS(b)],
              F_tr([(ps[:, b, c * 60:(c + 1) * 60], spst[:, c * 128:(c + 1) * 128], prmT[0:60, PC_ID:PC_ID + 60])
                    for c in range(8)]))
            for c in range(8):
                J(st_act, [PS(b)], AK(c * 528, (c + 1) * 528),
                  F_act(U[:, c, 0:320].rearrange("p (s e) -> p s e", s=4)[:, :, 1:16],
                        ps[:, b, c * 60:(c + 1) * 60].rearrange("p (s r) -> p s r", s=4), AF.Copy))

        if l == 0:
            prenorm_big(T, PC_GMP + 16 * l)

        def u_out(c):
            if is_p:
                return U[:, c, 16:16 + T]
            return U[:, c, 0:320].rearrange("p (s e) -> p s e", s=4)[:, :, 16:80]

        def psT(b):
            return ps[:, b, 0:T] if is_p else seg(ps[:, b, 0:T])

        def pooling():
            for g, w in enumerate(POOL_WINS):
                if is_p:
                    views = [(U[:, 2 * g:2 * g + 2, :], tmpA.rearrange("p (c e) -> p c e", c=2),
                              tmpB.rearrange("p (c e) -> p c e", c=2), dT[:, 2 * g:2 * g + 2, 0:T], 528,
                              AK(2 * g * 528, (2 * g + 2) * 528), [H(16 + 2 * g), H(17 + 2 * g)], 2 * g)]
                else:
                    views = []
                    for c in (2 * g, 2 * g + 1):
                        views.append((U[:, c, 0:320].rearrange("p (s e) -> p s e", s=4),
                                      tmpA[:, 0:320].rearrange("p (s e) -> p s e", s=4),
                                      tmpB[:, 0:320].rearrange("p (s e) -> p s e", s=4),
                                      seg(dT[:, c, 0:T]), 80, AK(c * 528, (c + 1) * 528), [H(16 + c)], c))
                for V_, tA, tB, dO, E, ukeys, dkeys, c0 in views:
                    J(st_dve, ukeys, ["tA"], F_tt(tA[:, :, 1:E], V_[:, :, 1:E], V_[:, :, 0:E - 1], ALU.add))
                    fin, oth, fk, ok = tA, tB, "tA", "tB"
                    if w >= 4:
                        J(st_dve, ["tA"], ["tB"], F_tt(tB[:, :, 3:E], tA[:, :, 3:E], tA[:, :, 1:E - 2], ALU.add))
                        fin, oth, fk, ok = tB, tA, "tB", "tA"
                    if w >= 8:
                        J(st_dve, ["tB"], ["tA"], F_tt(tA[:, :, 7:E], tB[:, :, 7:E], tB[:, :, 3:E - 4], ALU.add))
                        fin, oth, fk, ok = tA, tB, "tA", "tB"
                    if w >= 16:
                        J(st_dve, ["tA"], ["tB"], F_tt(tB[:, :, 15:E], tA[:, :, 15:E], tA[:, :, 7:E - 8], ALU.add))
                        fin, oth, fk, ok = tB, tA, "tB", "tA"
                    J(st_dve, [fk] + ukeys, dkeys,
                      F_stt(dO, fin[:, :, 16:E], 1.0 / w, V_[:, :, 16:E], ALU.mult, ALU.subtract))
                    if is_p and tno == 0:
                        rc = prmT[:, PC_RC + c0 * 16:PC_RC + (c0 + 2) * 16].rearrange("p (c t) -> p c t", c=2)
                        J(st_dve, [fk, "prm"], [ok], F_tt(oth[:, :, 0:16], fin[:, :, 16:32], rc, ALU.mult))
                        J(st_dve, [ok] + ukeys, dkeys, F_tt(dO[:, :, 0:16], oth[:, :, 0:16], V_[:, :, 16:32], ALU.subtract))
            if is_p and tno < 3:
                J(st_act, UKEYS, [("uh", l)], F_act(Uh[l][:, :, :], U[:, :, 512:528], AF.Copy))


        u_slots = []
        for panel in range(5):
            s = load_w(l, "in", wpanel_src(l, "in", 0, 16, panel * 512), lambda s_: wsl[s_][:, :, :])
            if panel < 2:
                u_slots.append(s)
            nchunk = 4 if panel < 4 else 2
            for m in range(nchunk):
                b = next_bank()
                prs = [(wsl[s][:, kc, m * 128:(m + 1) * 128], hn[:, kc, 0:T]) for kc in range(16)]
                if panel == 0 and m == 0:
                    mm_trailing(ps[:, b, 0:T], W(s), prs, b)
                else:
                    J(st_pe, [W(s)] + HNALL, [PS(b)], F_mm([(ps[:, b, 0:T], prs, True, True)]))
                if panel < 2:
                    c = panel * 4 + m
                    J(st_act, [PS(b)], AK(c * 528, (c + 1) * 528), F_act(u_out(c), psT(b), AF.Copy))
                elif panel < 4:
                    i = (panel - 2) * 4 + m
                    J(st_act, [PS(b)], [H(24 + i)], F_act(QT[:, i, 0:T], ps[:, b, 0:T], AF.Copy, scale=0.125))
                else:
                    gs = m
                    if is_p:
                        ko = KT[:, gs, 128:128 + T]
                    else:
                        ko = KT[:, gs, :].rearrange("p (s e) -> p s e", s=4)[:, :, 128:192]
                    J(st_act, [PS(b)], [("kt", gs)], F_act(ko, psT(b), AF.Copy))
            if panel == 1 and (last_p or not is_p):
                chunks = [7] if is_p else [0, 1, 2, 3]
                for cq in chunks:
                    for pn in range(2):
                        b = next_bank()
                        sl = u_slots[pn]
                        J(st_pe, [W(sl)] + HNALL, [PS(b)],
                          F_mm([(ps[0:64, b, 0:512],
                                 [(hn[:, kc, cq * 64:(cq + 1) * 64], wsl[sl][:, kc, :]) for kc in range(16)], True, True)]))
                        J(st_dve, [PS(b)], [A(14 + pn)], F_copy(ust[:, pn * 512:(pn + 1) * 512], ps[0:64, b, 0:512]))
                    dst = ppo[l, :, :] if is_p else pso[l, cq, :, :]
                    J(st_gp, [A(14), A(15)], [], F_dma([(dst, bufA[49:64, 7168:8192])], sem_u), dma=(sem_u, 1))
            if panel == 1:
                pooling()
            if panel == 4:
                def needK_of(cq_):
                    return (last_p and cq_ >= 6) or (not is_p)
                skip = set()
                for cq in range(NCH):
                    if cq in skip:
                        continue
                    needK = needK_of(cq)
                    if is_p and (not needK) and cq + 1 < NCH and not needK_of(cq + 1):
                        skip.add(cq + 1)
                        b = next_bank()
                        J(st_pe, [W(s)] + HNALL, [PS(b)],
                          F_mm([(ps[:, b, 0:256],
                                 [(hn[:, kc, cq * 64:(cq + 2) * 64], wsl[s][:, kc, 256:512]) for kc in range(16)], True, True)]))
                        for hh in range(2):
                            J(st_dve, [PS(b)], [("vp", 2 + cq + hh)],
                              F_copy(Vp[0:64, 2 + cq + hh, :, 0:64],
                                     ps[hh * 64:(hh + 1) * 64, b, 0:256].rearrange("p (g d) -> p g d", g=4)))
                        continue
                    kb_own = (2 + cq) if is_p else (3 * cq + 2)
                    b = next_bank()
                    c0 = 0 if needK else 256
                    N = 512 - c0
                    J(st_pe, [W(s)] + HNALL, [PS(b)],
                      F_mm([(ps[0:64, b, 0:N],
                             [(hn[:, kc, cq * 64:(cq + 1) * 64], wsl[s][:, kc, c0:512]) for kc in range(16)], True, True)]))
                    J(st_dve, [PS(b)], [("vp", kb_own)],
                      F_copy(Vp[0:64, kb_own, :, 0:64], ps[0:64, b, N - 256:N].rearrange("p (g d) -> p g d", g=4)))
                    if needK:
                        J(st_act, [PS(b)], [A(13)], F_act(kvst[:, :], ps[0:64, b, 0:512], AF.Copy))
                        if is_p:
                            r0 = (cq - 6) * 64
                            items = [(kpo[l, r0:r0 + 64, :], kvst[:, 0:256]), (vpo[l, r0:r0 + 64, :], kvst[:, 256:512])]
                        else:
                            items = [(kso[l, cq, 64:128, :], kvst[:, 0:256]), (vso[l, cq, 64:128, :], kvst[:, 256:512])]
                        J(st_gp, [A(13)], [], F_dma(items, sem_kv), dma=(sem_kv, 2))

        s = load_w(l, "pw", w_bf["pw"][l].rearrange("(j p) c -> p j c", p=128),
                   lambda s_: wsl[s_][:, 0:4, :].rearrange("p a (b c) -> p (a b) c", c=256))
        pwv = wsl[s][:, 0:4, :].rearrange("p a (b c) -> p (a b) c", c=256)
        for g in range(4):
            for mo in range(2):
                b = next_bank()
                J(st_pe, [W(s), H(16 + 2 * g), H(17 + 2 * g)], [PS(b)],
                  F_mm([(ps[:, b, 0:T], [(pwv[:, 2 * g + ki, mo * 128:(mo + 1) * 128], dT[:, 2 * g + ki, 0:T])
                                         for ki in range(2)], True, True)]))
                cc = 2 * g + mo
                J(st_act, [PS(b), "prm"], [H(cc)],
                  F_act(catT[:, cc, 0:T], ps[:, b, 0:T], AF.Copy, scale=prmT[:, PC_PS + 8 * l + cc:PC_PS + 8 * l + cc + 1]))

        jobs = []
        for c in range(NCH):
            if is_p:
                kbs = [kb for kb in (c, c + 1, c + 2) if not (tno == 0 and kb < 2)]
            else:
                kbs = [3 * c, 3 * c + 1, 3 * c + 2]
            for gs in range(2):
                for hf in range(2):
                    jobs.append((c, gs, hf, kbs))

        def qk(j):
            c, gs, hf, kbs = jobs[j]
            nk = len(kbs)
            base = 2 * (j % 3)
            groups = []
            for ki, kb in enumerate(kbs):
                o0 = base * 512 + ki * 256
                groups.append((psf[0:64, o0:o0 + 256],
                               [(KT[hf * 64:(hf + 1) * 64, gs, kb * 64:(kb + 1) * 64],
                                 QT[hf * 64:(hf + 1) * 64, 4 * gs:4 * gs + 4, c * 64:(c + 1) * 64])], True, True))
            J(st_pe, [("kt", gs)] + [H(24 + 4 * gs + i) for i in range(4)], [PS(base), PS(base + 1)], F_mm(groups))
            n = nk * 256
            J(st_act, [PS(base), PS(base + 1)], [("pt", j % 3)],
              F_act(PT[j % 3][:, 0:n], psf[0:64, base * 512:base * 512 + n], AF.Exp))

        def pv(j):
            c, gs, hf, kbs = jobs[j]
            nk = len(kbs)
            bo = 6 + (j % 2)
            g = 2 * gs + hf
            e0 = (2 * gs + hf) * 256
            groups = [(ps[:, bo, 0:256],
                       [(Vp[0:64, kb, g, :], PT[j % 3][:, ki * 256:(ki + 1) * 256]) for ki, kb in enumerate(kbs)]
                       + [(zo[:, :], esrow[:, e0:e0 + 256])],
                       True, True)]
            J(st_pe, [("pt", j % 3), "zo"] + [("vp", kb) for kb in kbs]
              + [("esb", h) for h in range(8 * gs + 4 * hf, 8 * gs + 4 * hf + 4)], [PS(bo)], F_mm(groups))
            ta, tb_ = t1[j % 2], t2[j % 2]
            J(st_act, [PS(bo)], ["t1"], F_act(ta[64:128, 0:256], ps[64:128, bo, 0:256], AF.Ln))
            J(st_act, ["t1"], [("t2", j % 2)], F_act(tb_[0:64, 0:256], ta[64:128, 0:256], AF.Exp, scale=-1.0))
            J(st_dve, [PS(bo), ("t2", j % 2)], [H(8 + 4 * gs + i) for i in range(4)],
              F_tt(catT[hf * 64:(hf + 1) * 64, 8 + 4 * gs:12 + 4 * gs, c * 64:(c + 1) * 64],
                   ps[0:64, bo, 0:256].rearrange("p (i q) -> p i q", i=4),
                   tb_[0:64, 0:256].rearrange("p (i q) -> p i q", i=4), ALU.mult))

        nj = len(jobs)
        qk(0)
        if nj > 1:
            qk(1)
        for j in range(nj):
            if j + 2 < nj:
                qk(j + 2)
            pv(j)
        if is_p and tno < 3:
            J(st_act, [("kt", 0), ("kt", 1)], [("kth", l)], F_act(KTh[l][:, :, :], KT[:, :, 512:640], AF.Copy))
            J(st_act, [("vp", 8), ("vp", 9)], [("vh", l)], F_act(Vh[l][:, :, :, :], Vp[:, 8:10, :, :], AF.Copy))

        gcol = PC_GMO + 16 * l
        rm = Rms(T)
        for panel in range(4):
            s = load_w(l, "out", wpanel_src(l, "out", 0, 16, panel * 512), lambda s_: wsl[s_][:, :, :])
            for m in range(4):
                co = 4 * panel + m
                b = next_bank()
                J(st_pe, [W(s)] + [H(k) for k in range(16)], [PS(b)],
                  F_mm([(ps[:, b, 0:T], [(wsl[s][:, kc, m * 128:(m + 1) * 128], catT[:, kc, 0:T]) for kc in range(16)],
                         True, True)]))
                J(st_act, [PS(b)], [H(28 + co)], F_act(sqH[:, co, 0:T], ps[:, b, 0:T], AF.Square))
                J(st_act, [PS(b), "prm"], [A(co)],
                  F_act(mix[:, co, 0:T], ps[:, b, 0:T], AF.Copy, scale=prmT[:, gcol + co:gcol + co + 1]))
                rm.acc_delayed(sqH[:, co, 0:T], H(28 + co))
        rm.finish()
        residual_update(T, mix, PC_GFP + 16 * l, use_pool)

        for grp in range(11):
            s_g = load_w(l, "gate", wpanel_src(l, "gate", 0, 16, grp * 512), lambda s_: wsl[s_][:, :, :])
            s_u = load_w(l, "up", wpanel_src(l, "up", 0, 16, grp * 512), lambda s_: wsl[s_][:, :, :])
            for m in range(4):
                b = next_bank()
                prs = [(wsl[s_g][:, kc, m * 128:(m + 1) * 128], hn[:, kc, 0:T]) for kc in range(16)]
                if grp == 0 and m == 0:
                    mm_trailing(ps[:, b, 0:T], W(s_g), prs, b)
                else:
                    J(st_pe, [W(s_g)] + HNALL, [PS(b)], F_mm([(ps[:, b, 0:T], prs, True, True)]))
                J(st_act, [PS(b)], [("sg", m)], F_act(sg[m][:, 0:T], ps[:, b, 0:T], AF.Silu))
            for m in range(4):
                k = 4 * grp + m
                b = next_bank()
                J(st_pe, [W(s_u)] + HNALL, [PS(b)],
                  F_mm([(ps[:, b, 0:T], [(wsl[s_u][:, kc, m * 128:(m + 1) * 128], hn[:, kc, 0:T]) for kc in range(16)],
                         True, True)]))
                J(st_dve, [PS(b), ("sg", m)], [H(k)], F_tt(hid[:, k, 0:T], sg[m][:, 0:T], ps[:, b, 0:T], ALU.mult))
        gcol = PC_GFO + 16 * l
        rm = Rms(T)
        for panel in range(4):
            banks = [next_bank() for _ in range(4)]
            for (k0, nk) in ((0, 16), (16, 16), (32, 12)):
                s = load_w(l, "down", wpanel_src(l, "down", k0, nk, panel * 512),
                           lambda s_, nk=nk: wsl[s_][:, 0:nk, :])
                for m in range(4):
                    b = banks[m]
                    J(st_pe, [W(s)] + [H(k0 + jj) for jj in range(nk)], [PS(b)],
                      F_mm([(ps[:, b, 0:T],
                             [(wsl[s][:, jj, m * 128:(m + 1) * 128], hid[:, k0 + jj, 0:T]) for jj in range(nk)],
                             k0 == 0, k0 + nk == 44)]))
            for m in range(4):
                co = 4 * panel + m
                b = banks[m]
                J(st_act, [PS(b)], [HN(co)], F_act(hn[:, co, 0:T], ps[:, b, 0:T], AF.Square))
                J(st_act, [PS(b), "prm"], [A(co)],
                  F_act(mix[:, co, 0:T], ps[:, b, 0:T], AF.Copy, scale=prmT[:, gcol + co:gcol + co + 1]))
                rm.acc_delayed(hn[:, co, 0:T], HN(co))
        rm.finish()
        residual_update(T, mix, (PC_GMP + 16 * (l + 1)) if l + 1 < NL else None, use_pool)

        if l == NL - 1:
            for tb in range(NTB):
                k = tb % 2
                for q in range(4):
                    b = next_bank()
                    J(st_pe, [X(4 * q + j) for j in range(4)] + ["prm"], [PS(b)],
                      F_tr([(ps[:, b, j * 128:(j + 1) * 128], xT[:, 4 * q + j, tb * 128:(tb + 1) * 128], ident)
                            for j in range(4)]))
                    J(st_act, [PS(b)], [HN(8 * k + 2 * q), HN(8 * k + 2 * q + 1)],
                      F_act(stg[k][:, q * 512:(q + 1) * 512], ps[:, b, :], AF.Copy))
                dst = yp[tno * 512 + tb * 128: tno * 512 + (tb + 1) * 128, :] if is_p else ys[tb * 128:(tb + 1) * 128, :]
                J(st_gp, [HN(8 * k + i) for i in range(8)], [], F_dma([(dst, stg[k])], sem_ys[k]), dma=(sem_ys[k], 1))

    tiles = [dict(kind="P", t=t, T=512) for t in range(4)] + [dict(kind="S", T=256)]

    def prologue_dma():
        J(st_gp, [], [], F_dma(cc_items, sem_cc), dma=(sem_cc, len(cc_items)))
        emit_casts()

    if tile_sel is not None:
        tiles = [tiles[i] for i in tile_sel]
    for ti, tile in enumerate(tiles):
        for l in range(NL):
            tile_layer(tile, l, after_xload=prologue_dma if (ti == 0 and l == 0) else None, use_pool=False)

    finals = []
    for sm in (sem_ys[0], sem_ys[1], sem_kv, sem_u, sem_cc):
        v = cx.dma_cnt.get(id(sm), 0)
        if v:
            finals.append((sm, v))

    with nc.Block() as block:
        @block.tensor
        def _(e):
            _run(st_pe, e)

        @block.scalar
        def _(e):
            _run(st_act, e)

        @block.vector
        def _(e):
            _run(st_dve, e)

        @block.sync
        def _(e):
            _run(st_sp, e)

        @block.gpsimd
        def _(e):
            _run(st_gp, e)
            for sm, v in finals:
                e.wait_ge(sm, v)

    es.close()
    return nc


_NC_CACHE = {}


def _get_nc():
    if "nc" not in _NC_CACHE:
        _NC_CACHE["nc"] = build_nc()
    return _NC_CACHE["nc"]


def _prep_params(inp):
    prm = np.zeros((128, NPRM), np.float32)

    def fm(g):
        return np.ascontiguousarray(np.asarray(g, np.float32).reshape(16, 128).T)

    for l in range(NL):
        prm[:, PC_GMP + 16 * l:PC_GMP + 16 * l + 16] = fm(inp["g_mix_pre"][l])
        prm[:, PC_GMO + 16 * l:PC_GMO + 16 * l + 16] = fm(inp["g_mix_post"][l])
        prm[:, PC_GFP + 16 * l:PC_GFP + 16 * l + 16] = fm(inp["g_ffn_pre"][l])
        prm[:, PC_GFO + 16 * l:PC_GFO + 16 * l + 16] = fm(inp["g_ffn_post"][l])
        prm[:, PC_PS + 8 * l:PC_PS + 8 * l + 8] = np.asarray(inp["pool_scale"][l], np.float32).reshape(8, 128).T
        sk = np.asarray(inp["attn_sinks"][l], np.float32)
        order = [A_IDX[4 * gs + i] + 4 * hf for gs in range(2) for hf in range(2) for i in range(4)]
        prm[:, PC_SK + 16 * l:PC_SK + 16 * l + 16] = sk[order][None, :]
    rc = np.zeros((8, 16), np.float32)
    for c in range(8):
        w = POOL_WINS[c // 2]
        for t in range(16):
            rc[c, t] = 1.0 / min(t + 1, w)
    prm[:, PC_RC:PC_RC + 128] = rc.reshape(1, 128)
    prm[:, PC_ID:PC_ID + 128] = np.eye(128, dtype=np.float32)
    return prm


def kernel(**inputs):
    inp = {k: np.asarray(v) for k, v in inputs.items()}
    n = 8
    qorder = [A_IDX[i] + 4 * hf for i in range(8) for hf in range(2)]
    w_in = inp["w_in"].astype(np.float32, copy=False)
    qcols = np.concatenate([np.arange(1024 + h * 64, 1024 + (h + 1) * 64) for h in qorder])
    cols = np.concatenate([np.arange(0, 1024), qcols, np.arange(2048, 2560)])
    w_in_p = np.ascontiguousarray(w_in[:, :, cols])
    rows = np.concatenate([np.arange(0, 1024), qcols])
    w_out_p = np.ascontiguousarray(inp["w_out"].astype(np.float32, copy=False)[:, rows, :])
    pool_w = np.ascontiguousarray(inp["pool_w"].astype(np.float32, copy=False).reshape(2, 1024, 256))
    prm = _prep_params(inp)
    shared = {
        "prm": prm, "w_in": w_in_p, "pool_w": pool_w, "w_out": w_out_p,
        "w_gate": np.ascontiguousarray(inp["w_gate"], dtype=np.float32),
        "w_up": np.ascontiguousarray(inp["w_up"], dtype=np.float32),
        "w_down": np.ascontiguousarray(inp["w_down"], dtype=np.float32),
    }
    in_maps = []
    for c in range(n):
        m = dict(shared)
        m["xp"] = np.ascontiguousarray(inp["x_prompt"][c], dtype=np.float32)
        m["xs"] = np.ascontiguousarray(inp["x_sample"][4 * c:4 * c + 4].reshape(256, D), dtype=np.float32)
        m["ck"] = np.ascontiguousarray(inp["cache_k"][:, 4 * c:4 * c + 4].reshape(2, 4, 128, 256), dtype=np.float32)
        m["cv"] = np.ascontiguousarray(inp["cache_v"][:, 4 * c:4 * c + 4].reshape(2, 4, 128, 256), dtype=np.float32)
        m["sp"] = np.ascontiguousarray(inp["state_pool"][:, 4 * c:4 * c + 4].reshape(2, 60, 1024), dtype=np.float32)
        in_maps.append(m)
    nc = _get_nc()
    res = run_bass_kernel_spmd(nc, in_maps, core_ids=list(range(n)))
    R = res.results
    y_prompt = np.stack([np.asarray(R[c]["yp"], np.float32) for c in range(n)], 0)
    y_sample = np.concatenate([np.asarray(R[c]["ys"], np.float32).reshape(4, 64, D) for c in range(n)], 0)
    kp = np.stack([np.asarray(R[c]["kpo"], np.float32).reshape(2, 128, 4, 64) for c in range(n)], 1)
    vp = np.stack([np.asarray(R[c]["vpo"], np.float32).reshape(2, 128, 4, 64) for c in range(n)], 1)
    pp = np.stack([np.asarray(R[c]["ppo"], np.float32) for c in range(n)], 1)
    ks = np.concatenate([np.asarray(R[c]["kso"], np.float32).reshape(2, 4, 128, 4, 64) for c in range(n)], 1)
    vs = np.concatenate([np.asarray(R[c]["vso"], np.float32).reshape(2, 4, 128, 4, 64) for c in range(n)], 1)
    pss = np.concatenate([np.asarray(R[c]["pso"], np.float32) for c in range(n)], 1)
    return (y_prompt, y_sample, kp, vp, pp, ks, vs, pss)
```

```python
from contextlib import ExitStack

import numpy as np
import concourse.bass as bass
import concourse.mybir as mybir
from concourse.bass_utils import run_bass_kernel_spmd

F32 = mybir.dt.float32
BF16 = mybir.dt.bfloat16
AF = mybir.ActivationFunctionType
ALU = mybir.AluOpType

D = 2048
DFF = 5632
NL = 2
EPS = 1e-6
A_IDX = [0, 1, 2, 3, 8, 9, 10, 11]
POOL_WINS = (2, 4, 8, 16)

PC_GMP, PC_GMO, PC_GFP, PC_GFO = 0, 32, 64, 96
PC_PS = 128
PC_SK = 144
PC_RC = 176
PC_ID = 304
NPRM = 432


class Stream:
    def __init__(self, name, sem, selfsync):
        self.name, self.sem, self.selfsync = name, sem, selfsync
        self.cnt = 0
        self.ops = []
        self.waited = {}


class Ctx:
    def __init__(self):
        self.last_w = {}
        self.readers = {}
        self.dma_cnt = {}

    def job(self, st, reads, writes, fn, dma=None):
        writes = list(writes) + [k for k in reads if k[0] == "ps"]
        reads = [k for k in reads if k[0] != "ps"]
        toks = {}

        def add(tok):
            if tok is None:
                return
            k = id(tok[0])
            if k not in toks or toks[k][1] < tok[1]:
                toks[k] = tok

        for k in reads:
            add(self.last_w.get(k))
        for k in writes:
            add(self.last_w.get(k))
            for t in self.readers.get(k, {}).values():
                add(t)
        for s, v in toks.values():
            if s is st.sem and not st.selfsync:
                continue
            if st.waited.get(id(s), 0) >= v:
                continue
            st.waited[id(s)] = v
            st.ops.append(("w", s, v))
        if dma is None:
            st.cnt += 1
            tok = (st.sem, st.cnt)
            st.ops.append(("e", fn, True))
        else:
            sem, n = dma
            self.dma_cnt[id(sem)] = self.dma_cnt.get(id(sem), 0) + 16 * n
            tok = (sem, self.dma_cnt[id(sem)])
            st.ops.append(("e", fn, False))
        for k in reads:
            r = self.readers.setdefault(k, {})
            o = r.get(id(tok[0]))
            if o is None or o[1] < tok[1]:
                r[id(tok[0])] = tok
        for k in writes:
            self.last_w[k] = tok
            self.readers[k] = {}
        return tok


def _run(st, e):
    for op in st.ops:
        if op[0] == "w":
            e.wait_ge(op[1], op[2])
        else:
            ins = op[1](e)
            if op[2]:
                ins.then_inc(st.sem, 1)


def build_nc(tile_sel=None):
    nc = bass.Bass("TRN2", target_bir_lowering=False)

    def din(name, shape):
        return nc.dram_tensor(name, shape, F32, kind="ExternalInput").ap()

    def dout(name, shape):
        return nc.dram_tensor(name, shape, F32, kind="ExternalOutput").ap()

    def dscr(name, shape):
        return nc.dram_tensor(name, shape, BF16, kind="Internal").ap()

    xp = din("xp", [2048, D])
    xs = din("xs", [256, D])
    ck = din("ck", [2, 4, 128, 256])
    cv = din("cv", [2, 4, 128, 256])
    sp = din("sp", [2, 60, 1024])
    prm = din("prm", [128, NPRM])
    w_src = {
        "in": din("w_in", [2, D, 2560]),
        "pw": din("pool_w", [2, 1024, 256]),
        "out": din("w_out", [2, D, D]),
        "gate": din("w_gate", [2, D, DFF]),
        "up": din("w_up", [2, D, DFF]),
        "down": din("w_down", [2, DFF, D]),
    }
    w_bf = {
        "in": dscr("b_in", [2, D, 2560]),
        "pw": dscr("b_pw", [2, 1024, 256]),
        "out": dscr("b_out", [2, D, D]),
        "gate": dscr("b_gate", [2, D, DFF]),
        "up": dscr("b_up", [2, D, DFF]),
        "down": dscr("b_down", [2, DFF, D]),
    }
    yp = dout("yp", [2048, D])
    ys = dout("ys", [256, D])
    kpo = dout("kpo", [2, 128, 256])
    vpo = dout("vpo", [2, 128, 256])
    ppo = dout("ppo", [2, 15, 1024])
    kso = dout("kso", [2, 4, 128, 256])
    vso = dout("vso", [2, 4, 128, 256])
    pso = dout("pso", [2, 4, 15, 1024])

    es = ExitStack()

    def sb(name, shape, dt):
        return es.enter_context(nc.sbuf_tensor(name, shape, dt))

    def sem(name):
        return es.enter_context(nc.semaphore(name))

    xT = sb("xT", [128, 16, 512], F32)
    hn_raw = sb("hn", [128, 4096], F32)
    wsl = [sb(f"ws{i}", [128, 16, 512], BF16) for i in range(2)]
    bufA = sb("bufA", [128, 8192], F32)
    hid = sb("hid", [128, 44, 512], BF16)
    KT = sb("KT", [128, 2, 768], BF16)
    Vp = sb("Vp", [64, 12, 4, 128], BF16)
    PT = [sb(f"PT{i}", [64, 768], BF16) for i in range(3)]
    t1 = [sb("t1_0", [128, 256], F32)] * 2
    tmpP = sb("tmpP", [128, 512], F32)
    t2 = [sb(f"t2_{i}", [64, 256], F32) for i in range(2)]
    rstd = sb("rstd", [128, 512], F32)
    sg = [sb(f"sg{i}", [128, 512], F32) for i in range(4)]
    prmT = sb("prmT", [128, NPRM], F32)
    onesm = sb("onesm", [128, 128], BF16)
    esrow = sb("esrow", [64, 1024], BF16)
    zo = sb("zo", [64, 128], BF16)
    es16 = sb("es16", [128, 16], F32)
    zt = sb("zt", [128, 64], F32)
    KTh = [sb(f"KTh{l}", [128, 2, 128], BF16) for l in range(2)]
    Vh = [sb(f"Vh{l}", [64, 2, 4, 128], BF16) for l in range(2)]
    Uh = [sb(f"Uh{l}", [128, 8, 16], F32) for l in range(2)]
    ps = es.enter_context(nc.psum_tensor("ps", [128, 8, 512], F32))
    psf = ps[:, :, :].rearrange("p b t -> p (b t)")

    hn = hn_raw[:, :].bitcast(BF16).rearrange("p (c t) -> p c t", c=16)
    stg = [hn_raw[:, 0:2048], hn_raw[:, 2048:4096]]
    hid_f = hid[:, :, :].rearrange("p c t -> p (c t)").bitcast(F32)
    xstg = [hid_f[:, 0:2048], hid_f[:, 2048:4096], hid_f[:, 4096:6144], bufA[:, 0:2048]]
    xstg_keys = [[("H", b) for b in range(0, 8)], [("H", b) for b in range(8, 16)], [("H", b) for b in range(16, 24)],
                 [("A", b) for b in range(0, 4)]]
    U = bufA[:, 0:4224].rearrange("p (c e) -> p c e", c=8)
    tmpA = bufA[:, 4224:5280]
    tmpB = bufA[:, 5280:6336]
    kvst = bufA[0:64, 6656:7168]
    ust = bufA[0:64, 7168:8192]
    spst = bufA[0:60, 7168:8192]
    ckst = bufA[:, 2560:3584].rearrange("p (s f) -> p s f", s=4)
    mix = bufA[:, :].rearrange("p (c t) -> p c t", c=16)
    catT = hid[:, 0:16, :]
    dT = hid[:, 16:24, :]
    QT = hid[:, 24:32, :]
    sqH = hid[:, 28:44, :]
    ident = prmT[:, PC_ID:PC_ID + 128]

    st_pe = Stream("pe", sem("s_pe"), False)
    st_act = Stream("act", sem("s_act"), True)
    st_dve = Stream("dve", sem("s_dve"), True)
    st_gp = Stream("gp", sem("s_gp"), True)
    st_sp = Stream("sp", sem("s_sp"), False)
    sem_w = [sem("s_w0"), sem("s_w1")]
    sem_xl = [sem(f"s_xl{i}") for i in range(4)]
    sem_ys = [sem("s_ys0"), sem("s_ys1")]
    sem_misc = sem("s_misc")
    sem_ck = sem("s_ck")
    sem_cv = sem("s_cv")
    sem_sp = sem("s_spl")
    sem_kv = sem("s_kv")
    sem_u = sem("s_u")
    sem_u2 = sem("s_u2")
    sem_cc = sem("s_cc")
    sem_cast = {(l, n): sem(f"s_c{l}{n}") for l in range(2) for n in w_src}

    cx = Ctx()
    J = cx.job

    def X(c):
        return ("x", c)

    def HN(c):
        return ("hn", c)

    def H(b):
        return ("H", b)

    def A(b):
        return ("A", b)

    def AK(lo, hi):
        return [A(b) for b in range(lo // 512, (hi - 1) // 512 + 1)]

    def PS(b):
        return ("ps", b)

    def W(s):
        return ("W", s)

    XALL = [X(c) for c in range(16)]
    HNALL = [HN(c) for c in range(16)]
    UKEYS = AK(0, 4224)

    bank_ctr = [0]

    def next_bank():
        b = bank_ctr[0] % 7
        bank_ctr[0] += 1
        return b

    slot_ctr = [0]

    def next_slot():
        s = slot_ctr[0] % 2
        slot_ctr[0] += 1
        return s

    def F_act(out, in_, func, **kw):
        return lambda e: e.activation(out=out, in_=in_, func=func, **kw)

    def F_tt(out, in0, in1, op):
        return lambda e: e.tensor_tensor(out=out, in0=in0, in1=in1, op=op)

    def F_stt(out, in0, scalar, in1, op0, op1):
        return lambda e: e.scalar_tensor_tensor(out=out, in0=in0, scalar=scalar, in1=in1, op0=op0, op1=op1)

    def F_copy(out, in_):
        return lambda e: e.tensor_copy(out=out, in_=in_)

    def F_memset(out, v):
        return lambda e: e.memset(out, v)

    def F_recip(out, in_):
        return lambda e: e.reciprocal(out=out, in_=in_)

    def F_mm(groups):
        def fn(e):
            ins = None
            for out, pairs, sf, sl in groups:
                n = len(pairs)
                for i, (lt, rh) in enumerate(pairs):
                    ins = e.matmul(out, lhsT=lt, rhs=rh, start=(sf and i == 0), stop=(sl and i == n - 1))
            return ins
        return fn

    def F_tr(items):
        def fn(e):
            ins = None
            for out, in_, idn in items:
                ins = e.transpose(out=out, in_=in_, identity=idn)
            return ins
        return fn

    def F_dma(items, s):
        def fn(e):
            ins = None
            for out, in_ in items:
                ins = e.dma_start(out=out, in_=in_)
                ins.then_inc(s, 16)
            return ins
        return fn

    J(st_gp, [], ["prm"], F_dma([(prmT[:, :], prm[:, :])], sem_misc), dma=(sem_misc, 1))
    J(st_dve, [], ["ones"], F_memset(onesm[:, :], 1.0 / 2048.0))
    J(st_dve, [], [("vp", kb) for kb in range(12)], F_memset(Vp[:, :, :, :], 1.0))
    J(st_dve, [], [A(b) for b in range(16)], F_memset(bufA[:, :], 0.0))
    J(st_dve, [], ["zt"], F_memset(zt[:, :], 0.0))
    J(st_dve, [], [("esb", h) for h in range(16)], F_memset(esrow[:, :], 0.0))
    J(st_dve, [], ["zo"], F_memset(zo[:, :], 0.0))
    J(st_dve, ["zo"], ["zo"], F_memset(zo[0:1, 64:128], 1.0))

    cc_items = []
    for l in range(2):
        cc_items.append((kso[l, :, 0:64, :], ck[l, :, 64:128, :]))
        cc_items.append((vso[l, :, 0:64, :], cv[l, :, 64:128, :]))

    def emit_casts():
        order = []
        for l in range(2):
            for n in ("in", "pw", "out", "gate", "up", "down"):
                order.append((l, n))
        for idx, (l, n) in enumerate(order):
            src, dst = w_src[n], w_bf[n]
            rows, cols = src.shape[1], src.shape[2]
            items = []
            csplit = 1
            while cols // csplit > 2048 or cols % csplit:
                csplit += 1
            cw = cols // csplit
            for r0 in range(0, rows, 512):
                r1 = min(rows, r0 + 512)
                for ci in range(csplit):
                    items.append((dst[l, r0:r1, ci * cw:(ci + 1) * cw], src[l, r0:r1, ci * cw:(ci + 1) * cw]))
            reads = []
            if idx >= 1:
                reads.append(("Wb",) + order[idx - 1])

            def fn(e, items=items, sm=sem_cast[(l, n)]):
                ins = None
                for i, (o, i_) in enumerate(items):
                    if i >= 2 and i % 2 == 0:
                        e.wait_ge(sm, 16 * i)
                    ins = e.dma_start(out=o, in_=i_)
                    ins.then_inc(sm, 16)
                return ins
            J(st_gp, reads, [("Wb", l, n)], fn, dma=(sem_cast[(l, n)], len(items)))

    def load_w(l, n, src_ap, dst_fn):
        s = next_slot()
        J(st_sp, [("Wb", l, n)], [W(s)], F_dma([(dst_fn(s), src_ap)], sem_w[s]), dma=(sem_w[s], 1))
        return s

    class Rms:
        def __init__(self, T):
            self.T, self.b, self.n, self.pend = T, 7, 0, []

        def acc(self, sq_chunk_ap, key):
            c = self.n
            self.n += 1
            J(st_pe, [key, "ones"], [PS(self.b)],
              F_mm([(ps[:, self.b, 0:self.T], [(onesm[:, :], sq_chunk_ap)], c == 0, c == 15)]))

        def acc_delayed(self, sq_chunk_ap, key, delay=2):
            self.pend.append((sq_chunk_ap, key))
            if len(self.pend) > delay:
                self.acc(*self.pend.pop(0))

        def finish(self):
            for it in self.pend:
                self.acc(*it)
            self.pend = []
            T, b = self.T, self.b
            J(st_act, [PS(b)], ["rstd"], F_act(rstd[:, 0:T], ps[:, b, 0:T], AF.Ln, bias=EPS))
            J(st_act, ["rstd"], ["rstd"], F_act(rstd[:, 0:T], rstd[:, 0:T], AF.Exp, scale=-0.5))

    POOL_RES = (11, 12, 13, 14, 15)
    POOL_HN = (13, 14, 15)

    def hn_from_x(T, gcol, use_pool=False):
        for c in range(16):
            if use_pool and c in POOL_HN:
                J(st_gp, [X(c), "prm"], ["tmpP"],
                  lambda e, c=c: e.tensor_scalar(out=tmpP[:, 0:T], in0=xT[:, c, 0:T],
                                                 scalar1=prmT[:, gcol + c:gcol + c + 1], scalar2=None, op0=ALU.mult))
                J(st_gp, ["tmpP", "rstd"], [HN(c)], F_tt(hn[:, c, 0:T], tmpP[:, 0:T], rstd[:, 0:T], ALU.mult))
        for c in range(16):
            if use_pool and c in POOL_HN:
                continue
            J(st_dve, [X(c), "rstd", "prm"], [HN(c)],
              F_stt(hn[:, c, 0:T], xT[:, c, 0:T], prmT[:, gcol + c:gcol + c + 1], rstd[:, 0:T], ALU.mult, ALU.mult))

    def prenorm_big(T, gcol):
        J(st_act, XALL, [H(b) for b in range(28, 44)], F_act(sqH[:, :, 0:T], xT[:, :, 0:T], AF.Square))
        r = Rms(T)
        for c in range(16):
            r.acc(sqH[:, c, 0:T], H(28 + c))
        r.finish()
        hn_from_x(T, gcol)

    def residual_update(T, src, next_gcol, use_pool):
        r = Rms(T) if next_gcol is not None else None
        for c in range(16):
            st_e = st_gp if (use_pool and c in POOL_RES) else st_dve
            J(st_e, [A(c), "rstd"], [A(c)], F_tt(src[:, c, 0:T], src[:, c, 0:T], rstd[:, 0:T], ALU.mult))
            J(st_e, [A(c), X(c)], [X(c)], F_tt(xT[:, c, 0:T], xT[:, c, 0:T], src[:, c, 0:T], ALU.add))
            if r is not None:
                J(st_act, [X(c)], [H(28 + c)], F_act(sqH[:, c, 0:T], xT[:, c, 0:T], AF.Square))
                r.acc(sqH[:, c, 0:T], H(28 + c))
        if r is not None:
            r.finish()
            hn_from_x(T, next_gcol, use_pool)

    def mm_trailing(out_ap, slot_key, pairs, bank):
        n = len(pairs)
        for kc, (lt, rh) in enumerate(pairs):
            J(st_pe, [slot_key, HN(kc)], [PS(bank)], F_mm([(out_ap, [(lt, rh)], kc == 0, kc == n - 1)]))

    def wpanel_src(l, n, k0, nk, c0):
        return w_bf[n][l].rearrange("(kc p) c -> p kc c", p=128)[:, k0:k0 + nk, c0:c0 + 512]

    def tile_layer(tile, l, after_xload=None, use_pool=False):
        kind, T = tile["kind"], tile["T"]
        NCH = T // 64
        NTB = T // 128
        tno = tile.get("t", 0)
        is_p = kind == "P"
        last_p = is_p and tno == 3

        def seg(ap2d_T):
            return ap2d_T.rearrange("p (s t) -> p s t", s=4)

        if l == 0:
            for tb in range(NTB):
                src = xp[tno * 512 + tb * 128: tno * 512 + (tb + 1) * 128, :] if is_p else xs[tb * 128:(tb + 1) * 128, :]
                skeys = xstg_keys[tb]
                J(st_sp, [], skeys, F_dma([(xstg[tb], src)], sem_xl[tb]), dma=(sem_xl[tb], 1))
            for tb in range(NTB):
                skeys = xstg_keys[tb]
                for q in range(4):
                    b = next_bank()
                    J(st_pe, skeys + ["prm"], [PS(b)],
                      F_tr([(ps[:, b, j * 128:(j + 1) * 128], xstg[tb][:, (4 * q + j) * 128:(4 * q + j + 1) * 128], ident)
                            for j in range(4)]))
                    J(st_act, [PS(b)], [X(4 * q + j) for j in range(4)],
                      F_act(xT[:, 4 * q:4 * q + 4, tb * 128:(tb + 1) * 128],
                            ps[:, b, :].rearrange("p (j t) -> p j t", j=4), AF.Copy))

        if after_xload is not None:
            after_xload()

        J(st_act, ["prm"], ["es16"], F_act(es16[:, :], prmT[:, PC_SK + 16 * l:PC_SK + 16 * l + 16], AF.Exp))
        for h in range(16):
            J(st_dve, ["es16", "zt"], [("esb", h)],
              lambda e, h=h: e.tensor_scalar(out=esrow[0:1, h * 64:(h + 1) * 64], in0=zt[0:1, :],
                                             scalar1=es16[0:1, h:h + 1], scalar2=None, op0=ALU.add))

        if is_p:
            if tno == 0:
                J(st_dve, [], UKEYS, F_memset(U[:, :, 0:16], 0.0))
            else:
                J(st_act, [("kth", l)], [("kt", 0), ("kt", 1)], F_act(KT[:, :, 0:128], KTh[l][:, :, :], AF.Copy))
                J(st_act, [("vh", l)], [("vp", 0), ("vp", 1)], F_act(Vp[:, 0:2, :, :], Vh[l][:, :, :, :], AF.Copy))
                J(st_act, [("uh", l)], UKEYS, F_act(U[:, :, 0:16], Uh[l][:, :, :], AF.Copy))
        else:
            J(st_gp, [], AK(2560, 3584), F_dma([(ckst, ck[l].rearrange("s t f -> t s f"))], sem_ck), dma=(sem_ck, 1))
            for gs in range(2):
                b = next_bank()
                J(st_pe, AK(2560, 3584) + ["prm"], [PS(b)],
                  F_tr([(ps[:, b, s_ * 128:(s_ + 1) * 128], ckst[:, s_, gs * 128:(gs + 1) * 128], ident) for s_ in range(4)]))
                J(st_act, [PS(b)], [("kt", gs)],
                  F_act(KT[:, gs, :].rearrange("p (s e) -> p s e", s=4)[:, :, 0:128],
                        ps[:, b, :].rearrange("p (s t) -> p s t", s=4), AF.Copy))
            items = []
            for s_ in range(4):
                for k2 in range(2):
                    items.append((Vp[0:64, 3 * s_ + k2, :, 0:64],
                                  cv[l, s_, k2 * 64:(k2 + 1) * 64, :].rearrange("p (g d) -> p g d", g=4)))
            J(st_gp, [], [("vp", 3 * s_ + k2) for s_ in range(4) for k2 in range(2)], F_dma(items, sem_cv),
              dma=(sem_cv, len(items)))
            J(st_gp, [], AK(7168, 8192), F_dma([(spst, sp[l, :, :])], sem_sp), dma=(sem_sp, 1))
            b = next_bank()
            J(st_pe, AK(7168, 8192) + ["prm"], [PS(b)],
              F_tr([(ps[:, b, c * 60:(c + 1) * 60], spst[:, c * 128:(c + 1) * 128], prmT[0:60, PC_ID:PC_ID + 60])
                    for c in range(8)]))
            for c in range(8):
                J(st_act, [PS(b)], AK(c * 528, (c + 1) * 528),
                  F_act(U[:, c, 0:320].rearrange("p (s e) -> p s e", s=4)[:, :, 1:16],
                        ps[:, b, c * 60:(c + 1) * 60].rearrange("p (s r) -> p s r", s=4), AF.Copy))

        if l == 0:
            prenorm_big(T, PC_GMP + 16 * l)

        def u_out(c):
            if is_p:
                return U[:, c, 16:16 + T]
            return U[:, c, 0:320].rearrange("p (s e) -> p s e", s=4)[:, :, 16:80]

        def psT(b):
            return ps[:, b, 0:T] if is_p else seg(ps[:, b, 0:T])

        def pooling():
            for g, w in enumerate(POOL_WINS):
                if is_p:
                    views = [(U[:, 2 * g:2 * g + 2, :], tmpA.rearrange("p (c e) -> p c e", c=2),
                              tmpB.rearrange("p (c e) -> p c e", c=2), dT[:, 2 * g:2 * g + 2, 0:T], 528,
                              AK(2 * g * 528, (2 * g + 2) * 528), [H(16 + 2 * g), H(17 + 2 * g)], 2 * g)]
                else:
                    views = []
                    for c in (2 * g, 2 * g + 1):
                        views.append((U[:, c, 0:320].rearrange("p (s e) -> p s e", s=4),
                                      tmpA[:, 0:320].rearrange("p (s e) -> p s e", s=4),
                                      tmpB[:, 0:320].rearrange("p (s e) -> p s e", s=4),
                                      seg(dT[:, c, 0:T]), 80, AK(c * 528, (c + 1) * 528), [H(16 + c)], c))
                for V_, tA, tB, dO, E, ukeys, dkeys, c0 in views:
                    J(st_dve, ukeys, ["tA"], F_tt(tA[:, :, 1:E], V_[:, :, 1:E], V_[:, :, 0:E - 1], ALU.add))
                    fin, oth, fk, ok = tA, tB, "tA", "tB"
                    if w >= 4:
                        J(st_dve, ["tA"], ["tB"], F_tt(tB[:, :, 3:E], tA[:, :, 3:E], tA[:, :, 1:E - 2], ALU.add))
                        fin, oth, fk, ok = tB, tA, "tB", "tA"
                    if w >= 8:
                        J(st_dve, ["tB"], ["tA"], F_tt(tA[:, :, 7:E], tB[:, :, 7:E], tB[:, :, 3:E - 4], ALU.add))
                        fin, oth, fk, ok = tA, tB, "tA", "tB"
                    if w >= 16:
                        J(st_dve, ["tA"], ["tB"], F_tt(tB[:, :, 15:E], tA[:, :, 15:E], tA[:, :, 7:E - 8], ALU.add))
                        fin, oth, fk, ok = tB, tA, "tB", "tA"
                    J(st_dve, [fk] + ukeys, dkeys,
                      F_stt(dO, fin[:, :, 16:E], 1.0 / w, V_[:, :, 16:E], ALU.mult, ALU.subtract))
                    if is_p and tno == 0:
                        rc = prmT[:, PC_RC + c0 * 16:PC_RC + (c0 + 2) * 16].rearrange("p (c t) -> p c t", c=2)
                        J(st_dve, [fk, "prm"], [ok], F_tt(oth[:, :, 0:16], fin[:, :, 16:32], rc, ALU.mult))
                        J(st_dve, [ok] + ukeys, dkeys, F_tt(dO[:, :, 0:16], oth[:, :, 0:16], V_[:, :, 16:32], ALU.subtract))
            if is_p and tno < 3:
                J(st_act, UKEYS, [("uh", l)], F_act(Uh[l][:, :, :], U[:, :, 512:528], AF.Copy))


        u_slots = []
        for panel in range(5):
            s = load_w(l, "in", wpanel_src(l, "in", 0, 16, panel * 512), lambda s_: wsl[s_][:, :, :])
            if panel < 2:
                u_slots.append(s)
            nchunk = 4 if panel < 4 else 2
            for m in range(nchunk):
                b = next_bank()
                prs = [(wsl[s][:, kc, m * 128:(m + 1) * 128], hn[:, kc, 0:T]) for kc in range(16)]
                if panel == 0 and m == 0:
                    mm_trailing(ps[:, b, 0:T], W(s), prs, b)
                else:
                    J(st_pe, [W(s)] + HNALL, [PS(b)], F_mm([(ps[:, b, 0:T], prs, True, True)]))
                if panel < 2:
                    c = panel * 4 + m
                    J(st_act, [PS(b)], AK(c * 528, (c + 1) * 528), F_act(u_out(c), psT(b), AF.Copy))
                elif panel < 4:
                    i = (panel - 2) * 4 + m
                    J(st_act, [PS(b)], [H(24 + i)], F_act(QT[:, i, 0:T], ps[:, b, 0:T], AF.Copy, scale=0.125))
                else:
                    gs = m
                    if is_p:
                        ko = KT[:, gs, 128:128 + T]
                    else:
                        ko = KT[:, gs, :].rearrange("p (s e) -> p s e", s=4)[:, :, 128:192]
                    J(st_act, [PS(b)], [("kt", gs)], F_act(ko, psT(b), AF.Copy))
            if panel == 1 and (last_p or not is_p):
                if is_p:
                    rounds = [([U[:, c, 16 + T - 15:16 + T] for c in range(8)], ppo[l, :, :])]
                else:
                    rounds = [([U[:, c, s_ * 80 + 65:s_ * 80 + 80] for c in range(8)], pso[l, s_, :, :]) for s_ in range(4)]
                for ri, (srcs, dst) in enumerate(rounds):
                    base = 7168 if ri % 2 == 0 else 5632
                    bk = base // 512
                    for half in range(2):
                        b = next_bank()
                        J(st_pe, AK(half * 4 * 528, (half * 4 + 4) * 528) + ["prm"], [PS(b)],
                          F_tr([(ps[0:15, b, j * 128:(j + 1) * 128], srcs[half * 4 + j], ident) for j in range(4)]))
                        J(st_dve, [PS(b)], [A(bk + half)],
                          F_copy(bufA[0:15, base + half * 512:base + (half + 1) * 512], ps[0:15, b, :]))
                    smu = sem_u if ri % 2 == 0 else sem_u2
                    J(st_gp, [A(bk), A(bk + 1)], [], F_dma([(dst, bufA[0:15, base:base + 1024])], smu), dma=(smu, 1))
            if panel == 1:
                pooling()
            if panel == 4:
                def needK_of(cq_):
                    return (last_p and cq_ >= 6) or (not is_p)
                skip = set()
                for cq in range(NCH):
                    if cq in skip:
                        continue
                    needK = needK_of(cq)
                    if is_p and (not needK) and cq + 1 < NCH and not needK_of(cq + 1):
                        skip.add(cq + 1)
                        b = next_bank()
                        J(st_pe, [W(s)] + HNALL, [PS(b)],
                          F_mm([(ps[:, b, 0:256],
                                 [(hn[:, kc, cq * 64:(cq + 2) * 64], wsl[s][:, kc, 256:512]) for kc in range(16)], True, True)]))
                        for hh in range(2):
                            J(st_dve, [PS(b)], [("vp", 2 + cq + hh)],
                              F_copy(Vp[0:64, 2 + cq + hh, :, 0:64],
                                     ps[hh * 64:(hh + 1) * 64, b, 0:256].rearrange("p (g d) -> p g d", g=4)))
                        continue
                    kb_own = (2 + cq) if is_p else (3 * cq + 2)
                    b = next_bank()
                    c0 = 0 if needK else 256
                    N = 512 - c0
                    J(st_pe, [W(s)] + HNALL, [PS(b)],
                      F_mm([(ps[0:64, b, 0:N],
                             [(hn[:, kc, cq * 64:(cq + 1) * 64], wsl[s][:, kc, c0:512]) for kc in range(16)], True, True)]))
                    J(st_dve, [PS(b)], [("vp", kb_own)],
                      F_copy(Vp[0:64, kb_own, :, 0:64], ps[0:64, b, N - 256:N].rearrange("p (g d) -> p g d", g=4)))
                    if needK:
                        J(st_act, [PS(b)], [A(13)], F_act(kvst[:, :], ps[0:64, b, 0:512], AF.Copy))
                        if is_p:
                            r0 = (cq - 6) * 64
                            items = [(kpo[l, r0:r0 + 64, :], kvst[:, 0:256]), (vpo[l, r0:r0 + 64, :], kvst[:, 256:512])]
                        else:
                            items = [(kso[l, cq, 64:128, :], kvst[:, 0:256]), (vso[l, cq, 64:128, :], kvst[:, 256:512])]
                        J(st_gp, [A(13)], [], F_dma(items, sem_kv), dma=(sem_kv, 2))

        s = load_w(l, "pw", w_bf["pw"][l].rearrange("(j p) c -> p j c", p=128),
                   lambda s_: wsl[s_][:, 0:4, :].rearrange("p a (b c) -> p (a b) c", c=256))
        pwv = wsl[s][:, 0:4, :].rearrange("p a (b c) -> p (a b) c", c=256)
        for g in range(4):
            for mo in range(2):
                b = next_bank()
                J(st_pe, [W(s), H(16 + 2 * g), H(17 + 2 * g)], [PS(b)],
                  F_mm([(ps[:, b, 0:T], [(pwv[:, 2 * g + ki, mo * 128:(mo + 1) * 128], dT[:, 2 * g + ki, 0:T])
                                         for ki in range(2)], True, True)]))
                cc = 2 * g + mo
                J(st_act, [PS(b), "prm"], [H(cc)],
                  F_act(catT[:, cc, 0:T], ps[:, b, 0:T], AF.Copy, scale=prmT[:, PC_PS + 8 * l + cc:PC_PS + 8 * l + cc + 1]))

        jobs = []
        for c in range(NCH):
            if is_p:
                kbs = [kb for kb in (c, c + 1, c + 2) if not (tno == 0 and kb < 2)]
            else:
                kbs = [3 * c, 3 * c + 1, 3 * c + 2]
            for gs in range(2):
                for hf in range(2):
                    jobs.append((c, gs, hf, kbs))

        def qk(j):
            c, gs, hf, kbs = jobs[j]
            nk = len(kbs)
            base = 2 * (j % 3)
            groups = []
            for ki, kb in enumerate(kbs):
                o0 = base * 512 + ki * 256
                groups.append((psf[0:64, o0:o0 + 256],
                               [(KT[hf * 64:(hf + 1) * 64, gs, kb * 64:(kb + 1) * 64],
                                 QT[hf * 64:(hf + 1) * 64, 4 * gs:4 * gs + 4, c * 64:(c + 1) * 64])], True, True))
            J(st_pe, [("kt", gs)] + [H(24 + 4 * gs + i) for i in range(4)], [PS(base), PS(base + 1)], F_mm(groups))
            n = nk * 256
            J(st_act, [PS(base), PS(base + 1)], [("pt", j % 3)],
              F_act(PT[j % 3][:, 0:n], psf[0:64, base * 512:base * 512 + n], AF.Exp))

        def pv(j):
            c, gs, hf, kbs = jobs[j]
            nk = len(kbs)
            bo = 6 + (j % 2)
            g = 2 * gs + hf
            e0 = (2 * gs + hf) * 256
            groups = [(ps[:, bo, 0:256],
                       [(Vp[0:64, kb, g, :], PT[j % 3][:, ki * 256:(ki + 1) * 256]) for ki, kb in enumerate(kbs)]
                       + [(zo[:, :], esrow[:, e0:e0 + 256])],
                       True, True)]
            J(st_pe, [("pt", j % 3), "zo"] + [("vp", kb) for kb in kbs]
              + [("esb", h) for h in range(8 * gs + 4 * hf, 8 * gs + 4 * hf + 4)], [PS(bo)], F_mm(groups))
            ta, tb_ = t1[j % 2], t2[j % 2]
            if (j % 5) in (1, 3):
                J(st_dve, [PS(bo)], [("t2", j % 2)], F_recip(tb_[0:64, 0:256], ps[64:128, bo, 0:256]))
            else:
                J(st_act, [PS(bo)], ["t1"], F_act(ta[64:128, 0:256], ps[64:128, bo, 0:256], AF.Ln))
                J(st_act, ["t1"], [("t2", j % 2)], F_act(tb_[0:64, 0:256], ta[64:128, 0:256], AF.Exp, scale=-1.0))
            J(st_dve, [PS(bo), ("t2", j % 2)], [H(8 + 4 * gs + i) for i in range(4)],
              F_tt(catT[hf * 64:(hf + 1) * 64, 8 + 4 * gs:12 + 4 * gs, c * 64:(c + 1) * 64],
                   ps[0:64, bo, 0:256].rearrange("p (i q) -> p i q", i=4),
                   tb_[0:64, 0:256].rearrange("p (i q) -> p i q", i=4), ALU.mult))

        nj = len(jobs)
        qk(0)
        if nj > 1:
            qk(1)
        for j in range(nj):
            if j + 2 < nj:
                qk(j + 2)
            pv(j)
        if is_p and tno < 3:
            J(st_act, [("kt", 0), ("kt", 1)], [("kth", l)], F_act(KTh[l][:, :, :], KT[:, :, 512:640], AF.Copy))
            J(st_act, [("vp", 8), ("vp", 9)], [("vh", l)], F_act(Vh[l][:, :, :, :], Vp[:, 8:10, :, :], AF.Copy))

        gcol = PC_GMO + 16 * l
        rm = Rms(T)
        for panel in range(4):
            s = load_w(l, "out", wpanel_src(l, "out", 0, 16, panel * 512), lambda s_: wsl[s_][:, :, :])
            for m in range(4):
                co = 4 * panel + m
                b = next_bank()
                J(st_pe, [W(s)] + [H(k) for k in range(16)], [PS(b)],
                  F_mm([(ps[:, b, 0:T], [(wsl[s][:, kc, m * 128:(m + 1) * 128], catT[:, kc, 0:T]) for kc in range(16)],
                         True, True)]))
                J(st_act, [PS(b)], [H(28 + co)], F_act(sqH[:, co, 0:T], ps[:, b, 0:T], AF.Square))
                J(st_act, [PS(b), "prm"], [A(co)],
                  F_act(mix[:, co, 0:T], ps[:, b, 0:T], AF.Copy, scale=prmT[:, gcol + co:gcol + co + 1]))
                rm.acc_delayed(sqH[:, co, 0:T], H(28 + co))
        rm.finish()
        residual_update(T, mix, PC_GFP + 16 * l, use_pool)

        for grp in range(11):
            s_g = load_w(l, "gate", wpanel_src(l, "gate", 0, 16, grp * 512), lambda s_: wsl[s_][:, :, :])
            s_u = load_w(l, "up", wpanel_src(l, "up", 0, 16, grp * 512), lambda s_: wsl[s_][:, :, :])
            for m in range(4):
                b = next_bank()
                prs = [(wsl[s_g][:, kc, m * 128:(m + 1) * 128], hn[:, kc, 0:T]) for kc in range(16)]
                if grp == 0 and m == 0:
                    mm_trailing(ps[:, b, 0:T], W(s_g), prs, b)
                else:
                    J(st_pe, [W(s_g)] + HNALL, [PS(b)], F_mm([(ps[:, b, 0:T], prs, True, True)]))
                J(st_act, [PS(b)], [("sg", m)], F_act(sg[m][:, 0:T], ps[:, b, 0:T], AF.Silu))
            for m in range(4):
                k = 4 * grp + m
                b = next_bank()
                J(st_pe, [W(s_u)] + HNALL, [PS(b)],
                  F_mm([(ps[:, b, 0:T], [(wsl[s_u][:, kc, m * 128:(m + 1) * 128], hn[:, kc, 0:T]) for kc in range(16)],
                         True, True)]))
                J(st_dve, [PS(b), ("sg", m)], [H(k)], F_tt(hid[:, k, 0:T], sg[m][:, 0:T], ps[:, b, 0:T], ALU.mult))
        gcol = PC_GFO + 16 * l
        rm = Rms(T)
        for panel in range(4):
            banks = [next_bank() for _ in range(4)]
            for (k0, nk) in ((0, 16), (16, 16), (32, 12)):
                s = load_w(l, "down", wpanel_src(l, "down", k0, nk, panel * 512),
                           lambda s_, nk=nk: wsl[s_][:, 0:nk, :])
                for m in range(4):
                    b = banks[m]
                    J(st_pe, [W(s)] + [H(k0 + jj) for jj in range(nk)], [PS(b)],
                      F_mm([(ps[:, b, 0:T],
                             [(wsl[s][:, jj, m * 128:(m + 1) * 128], hid[:, k0 + jj, 0:T]) for jj in range(nk)],
                             k0 == 0, k0 + nk == 44)]))
            for m in range(4):
                co = 4 * panel + m
                b = banks[m]
                J(st_act, [PS(b)], [HN(co)], F_act(hn[:, co, 0:T], ps[:, b, 0:T], AF.Square))
                J(st_act, [PS(b), "prm"], [A(co)],
                  F_act(mix[:, co, 0:T], ps[:, b, 0:T], AF.Copy, scale=prmT[:, gcol + co:gcol + co + 1]))
                rm.acc_delayed(hn[:, co, 0:T], HN(co))
        rm.finish()
        residual_update(T, mix, (PC_GMP + 16 * (l + 1)) if l + 1 < NL else None, use_pool)

        if l == NL - 1:
            for tb in range(NTB):
                k = tb % 2
                for q in range(4):
                    b = next_bank()
                    J(st_pe, [X(4 * q + j) for j in range(4)] + ["prm"], [PS(b)],
                      F_tr([(ps[:, b, j * 128:(j + 1) * 128], xT[:, 4 * q + j, tb * 128:(tb + 1) * 128], ident)
                            for j in range(4)]))
                    J(st_act, [PS(b)], [HN(8 * k + 2 * q), HN(8 * k + 2 * q + 1)],
                      F_act(stg[k][:, q * 512:(q + 1) * 512], ps[:, b, :], AF.Copy))
                dst = yp[tno * 512 + tb * 128: tno * 512 + (tb + 1) * 128, :] if is_p else ys[tb * 128:(tb + 1) * 128, :]
                J(st_gp, [HN(8 * k + i) for i in range(8)], [], F_dma([(dst, stg[k])], sem_ys[k]), dma=(sem_ys[k], 1))

    tiles = [dict(kind="P", t=t, T=512) for t in range(4)] + [dict(kind="S", T=256)]

    def prologue_dma():
        J(st_gp, [], [], F_dma(cc_items, sem_cc), dma=(sem_cc, len(cc_items)))
        emit_casts()

    if tile_sel is not None:
        tiles = [tiles[i] for i in tile_sel]
    for ti, tile in enumerate(tiles):
        for l in range(NL):
            tile_layer(tile, l, after_xload=prologue_dma if (ti == 0 and l == 0) else None, use_pool=False)

    finals = []
    for sm in (sem_ys[0], sem_ys[1], sem_kv, sem_u, sem_u2, sem_cc):
        v = cx.dma_cnt.get(id(sm), 0)
        if v:
            finals.append((sm, v))

    with nc.Block() as block:
        @block.tensor
        def _(e):
            _run(st_pe, e)

        @block.scalar
        def _(e):
            _run(st_act, e)

        @block.vector
        def _(e):
            _run(st_dve, e)

        @block.sync
        def _(e):
            _run(st_sp, e)

        @block.gpsimd
        def _(e):
            _run(st_gp, e)
            for sm, v in finals:
                e.wait_ge(sm, v)

    es.close()
    return nc


_NC_CACHE = {}


def _get_nc():
    if "nc" not in _NC_CACHE:
        _NC_CACHE["nc"] = build_nc()
    return _NC_CACHE["nc"]


def _prep_params(inp):
    prm = np.zeros((128, NPRM), np.float32)

    def fm(g):
        return np.ascontiguousarray(np.asarray(g, np.float32).reshape(16, 128).T)

    for l in range(NL):
        prm[:, PC_GMP + 16 * l:PC_GMP + 16 * l + 16] = fm(inp["g_mix_pre"][l])
        prm[:, PC_GMO + 16 * l:PC_GMO + 16 * l + 16] = fm(inp["g_mix_post"][l])
        prm[:, PC_GFP + 16 * l:PC_GFP + 16 * l + 16] = fm(inp["g_ffn_pre"][l])
        prm[:, PC_GFO + 16 * l:PC_GFO + 16 * l + 16] = fm(inp["g_ffn_post"][l])
        prm[:, PC_PS + 8 * l:PC_PS + 8 * l + 8] = np.asarray(inp["pool_scale"][l], np.float32).reshape(8, 128).T
        sk = np.asarray(inp["attn_sinks"][l], np.float32)
        order = [A_IDX[4 * gs + i] + 4 * hf for gs in range(2) for hf in range(2) for i in range(4)]
        prm[:, PC_SK + 16 * l:PC_SK + 16 * l + 16] = sk[order][None, :]
    rc = np.zeros((8, 16), np.float32)
    for c in range(8):
        w = POOL_WINS[c // 2]
        for t in range(16):
            rc[c, t] = 1.0 / min(t + 1, w)
    prm[:, PC_RC:PC_RC + 128] = rc.reshape(1, 128)
    prm[:, PC_ID:PC_ID + 128] = np.eye(128, dtype=np.float32)
    return prm


def kernel(**inputs):
    inp = {k: np.asarray(v) for k, v in inputs.items()}
    n = 8
    qorder = [A_IDX[i] + 4 * hf for i in range(8) for hf in range(2)]
    w_in = inp["w_in"].astype(np.float32, copy=False)
    qcols = np.concatenate([np.arange(1024 + h * 64, 1024 + (h + 1) * 64) for h in qorder])
    cols = np.concatenate([np.arange(0, 1024), qcols, np.arange(2048, 2560)])
    w_in_p = np.ascontiguousarray(w_in[:, :, cols])
    rows = np.concatenate([np.arange(0, 1024), qcols])
    w_out_p = np.ascontiguousarray(inp["w_out"].astype(np.float32, copy=False)[:, rows, :])
    pool_w = np.ascontiguousarray(inp["pool_w"].astype(np.float32, copy=False).reshape(2, 1024, 256))
    prm = _prep_params(inp)
    shared = {
        "prm": prm, "w_in": w_in_p, "pool_w": pool_w, "w_out": w_out_p,
        "w_gate": np.ascontiguousarray(inp["w_gate"], dtype=np.float32),
        "w_up": np.ascontiguousarray(inp["w_up"], dtype=np.float32),
        "w_down": np.ascontiguousarray(inp["w_down"], dtype=np.float32),
    }
    in_maps = []
    for c in range(n):
        m = dict(shared)
        m["xp"] = np.ascontiguousarray(inp["x_prompt"][c], dtype=np.float32)
        m["xs"] = np.ascontiguousarray(inp["x_sample"][4 * c:4 * c + 4].reshape(256, D), dtype=np.float32)
        m["ck"] = np.ascontiguousarray(inp["cache_k"][:, 4 * c:4 * c + 4].reshape(2, 4, 128, 256), dtype=np.float32)
        m["cv"] = np.ascontiguousarray(inp["cache_v"][:, 4 * c:4 * c + 4].reshape(2, 4, 128, 256), dtype=np.float32)
        m["sp"] = np.ascontiguousarray(inp["state_pool"][:, 4 * c:4 * c + 4].reshape(2, 60, 1024), dtype=np.float32)
        in_maps.append(m)
    nc = _get_nc()
    res = run_bass_kernel_spmd(nc, in_maps, core_ids=list(range(n)))
    R = res.results
    y_prompt = np.stack([np.asarray(R[c]["yp"], np.float32) for c in range(n)], 0)
    y_sample = np.concatenate([np.asarray(R[c]["ys"], np.float32).reshape(4, 64, D) for c in range(n)], 0)
    kp = np.stack([np.asarray(R[c]["kpo"], np.float32).reshape(2, 128, 4, 64) for c in range(n)], 1)
    vp = np.stack([np.asarray(R[c]["vpo"], np.float32).reshape(2, 128, 4, 64) for c in range(n)], 1)
    pp = np.stack([np.asarray(R[c]["ppo"], np.float32) for c in range(n)], 1)
    ks = np.concatenate([np.asarray(R[c]["kso"], np.float32).reshape(2, 4, 128, 4, 64) for c in range(n)], 1)
    vs = np.concatenate([np.asarray(R[c]["vso"], np.float32).reshape(2, 4, 128, 4, 64) for c in range(n)], 1)
    pss = np.concatenate([np.asarray(R[c]["pso"], np.float32) for c in range(n)], 1)
    return (y_prompt, y_sample, kp, vp, pp, ks, vs, pss)
```

```python
from contextlib import ExitStack

import numpy as np
import concourse.bass as bass
import concourse.mybir as mybir
from concourse.bass_utils import run_bass_kernel_spmd

F32 = mybir.dt.float32
BF16 = mybir.dt.bfloat16
AF = mybir.ActivationFunctionType
ALU = mybir.AluOpType

D = 2048
DFF = 5632
NL = 2
EPS = 1e-6
A_IDX = [0, 1, 2, 3, 8, 9, 10, 11]
POOL_WINS = (2, 4, 8, 16)

PC_GMP, PC_GMO, PC_GFP, PC_GFO = 0, 32, 64, 96
PC_PS = 128
PC_SK = 144
PC_RC = 176
PC_ID = 304
NPRM = 432


class Stream:
    def __init__(self, name, sem, selfsync):
        self.name, self.sem, self.selfsync = name, sem, selfsync
        self.cnt = 0
        self.ops = []
        self.waited = {}


class Ctx:
    def __init__(self):
        self.last_w = {}
        self.readers = {}
        self.dma_cnt = {}

    def job(self, st, reads, writes, fn, dma=None):
        writes = list(writes) + [k for k in reads if k[0] == "ps"]
        reads = [k for k in reads if k[0] != "ps"]
        toks = {}

        def add(tok):
            if tok is None:
                return
            k = id(tok[0])
            if k not in toks or toks[k][1] < tok[1]:
                toks[k] = tok

        for k in reads:
            add(self.last_w.get(k))
        for k in writes:
            add(self.last_w.get(k))
            for t in self.readers.get(k, {}).values():
                add(t)
        for s, v in toks.values():
            if s is st.sem and not st.selfsync:
                continue
            if st.waited.get(id(s), 0) >= v:
                continue
            st.waited[id(s)] = v
            st.ops.append(("w", s, v))
        if dma is None:
            st.cnt += 1
            tok = (st.sem, st.cnt)
            st.ops.append(("e", fn, True))
        else:
            sem, n = dma
            self.dma_cnt[id(sem)] = self.dma_cnt.get(id(sem), 0) + 16 * n
            tok = (sem, self.dma_cnt[id(sem)])
            st.ops.append(("e", fn, False))
        for k in reads:
            r = self.readers.setdefault(k, {})
            o = r.get(id(tok[0]))
            if o is None or o[1] < tok[1]:
                r[id(tok[0])] = tok
        for k in writes:
            self.last_w[k] = tok
            self.readers[k] = {}
        return tok


def _run(st, e):
    for op in st.ops:
        if op[0] == "w":
            e.wait_ge(op[1], op[2])
        else:
            ins = op[1](e)
            if op[2]:
                ins.then_inc(st.sem, 1)


def build_nc(tile_sel=None):
    nc = bass.Bass("TRN2", target_bir_lowering=False)

    def din(name, shape):
        return nc.dram_tensor(name, shape, F32, kind="ExternalInput").ap()

    def dout(name, shape):
        return nc.dram_tensor(name, shape, F32, kind="ExternalOutput").ap()

    def dscr(name, shape):
        return nc.dram_tensor(name, shape, BF16, kind="Internal").ap()

    xp = din("xp", [2048, D])
    xs = din("xs", [256, D])
    ck = din("ck", [2, 4, 128, 256])
    cv = din("cv", [2, 4, 128, 256])
    sp = din("sp", [2, 60, 1024])
    prm = din("prm", [128, NPRM])
    w_src = {
        "in": din("w_in", [2, D, 2560]),
        "pw": din("pool_w", [2, 1024, 256]),
        "out": din("w_out", [2, D, D]),
        "gate": din("w_gate", [2, D, DFF]),
        "up": din("w_up", [2, D, DFF]),
        "down": din("w_down", [2, DFF, D]),
    }
    w_bf = {
        "in": dscr("b_in", [2, D, 2560]),
        "pw": dscr("b_pw", [2, 1024, 256]),
        "out": dscr("b_out", [2, D, D]),
        "gate": dscr("b_gate", [2, D, DFF]),
        "up": dscr("b_up", [2, D, DFF]),
        "down": dscr("b_down", [2, DFF, D]),
    }
    yp = dout("yp", [2048, D])
    ys = dout("ys", [256, D])
    kpo = dout("kpo", [2, 128, 256])
    vpo = dout("vpo", [2, 128, 256])
    ppo = dout("ppo", [2, 15, 1024])
    kso = dout("kso", [2, 4, 128, 256])
    vso = dout("vso", [2, 4, 128, 256])
    pso = dout("pso", [2, 4, 15, 1024])

    es = ExitStack()

    def sb(name, shape, dt):
        return es.enter_context(nc.sbuf_tensor(name, shape, dt))

    def sem(name):
        return es.enter_context(nc.semaphore(name))

    xT = sb("xT", [128, 16, 512], F32)
    hn_raw = sb("hn", [128, 4096], F32)
    wsl = [sb(f"ws{i}", [128, 16, 512], BF16) for i in range(2)]
    bufA = sb("bufA", [128, 8192], F32)
    hid = sb("hid", [128, 44, 512], BF16)
    KT = sb("KT", [128, 2, 768], BF16)
    Vp = sb("Vp", [64, 12, 4, 128], BF16)
    PT = [sb(f"PT{i}", [64, 768], BF16) for i in range(3)]
    t1 = [sb("t1_0", [128, 256], F32)] * 2
    tmpP = sb("tmpP", [128, 512], F32)
    t2 = [sb(f"t2_{i}", [64, 256], F32) for i in range(2)]
    rstd = sb("rstd", [128, 512], F32)
    sg = [sb(f"sg{i}", [128, 512], F32) for i in range(4)]
    prmT = sb("prmT", [128, NPRM], F32)
    onesm = sb("onesm", [128, 128], BF16)
    esrow = sb("esrow", [64, 1024], BF16)
    zo = sb("zo", [64, 128], BF16)
    es16 = sb("es16", [128, 16], F32)
    zt = sb("zt", [128, 64], F32)
    KTh = [sb(f"KTh{l}", [128, 2, 128], BF16) for l in range(2)]
    Vh = [sb(f"Vh{l}", [64, 2, 4, 128], BF16) for l in range(2)]
    Uh = [sb(f"Uh{l}", [128, 8, 16], F32) for l in range(2)]
    ps = es.enter_context(nc.psum_tensor("ps", [128, 8, 512], F32))
    psf = ps[:, :, :].rearrange("p b t -> p (b t)")

    hn = hn_raw[:, :].bitcast(BF16).rearrange("p (c t) -> p c t", c=16)
    stg = [hn_raw[:, 0:2048], hn_raw[:, 2048:4096]]
    hid_f = hid[:, :, :].rearrange("p c t -> p (c t)").bitcast(F32)
    xstg = [hid_f[:, 0:2048], hid_f[:, 2048:4096], hid_f[:, 4096:6144], bufA[:, 0:2048]]
    xstg_keys = [[("H", b) for b in range(0, 8)], [("H", b) for b in range(8, 16)], [("H", b) for b in range(16, 24)],
                 [("A", b) for b in range(0, 4)]]
    U = bufA[:, 0:4224].rearrange("p (c e) -> p c e", c=8)
    tmpA = bufA[:, 4224:5280]
    tmpB = bufA[:, 5280:6336]
    kvst = bufA[0:64, 6656:7168]
    ust = bufA[0:64, 7168:8192]
    spst = bufA[0:60, 7168:8192]
    ckst = bufA[:, 2560:3584].rearrange("p (s f) -> p s f", s=4)
    mix = bufA[:, :].rearrange("p (c t) -> p c t", c=16)
    catT = hid[:, 0:16, :]
    dT = hid[:, 16:24, :]
    QT = hid[:, 24:32, :]
    sqH = hid[:, 28:44, :]
    ident = prmT[:, PC_ID:PC_ID + 128]

    st_pe = Stream("pe", sem("s_pe"), False)
    st_act = Stream("act", sem("s_act"), True)
    st_dve = Stream("dve", sem("s_dve"), True)
    st_gp = Stream("gp", sem("s_gp"), True)
    st_sp = Stream("sp", sem("s_sp"), False)
    sem_w = [sem("s_w0"), sem("s_w1")]
    sem_xl = [sem(f"s_xl{i}") for i in range(4)]
    sem_ys = [sem("s_ys0"), sem("s_ys1")]
    sem_misc = sem("s_misc")
    sem_ck = sem("s_ck")
    sem_cv = sem("s_cv")
    sem_sp = sem("s_spl")
    sem_kv = sem("s_kv")
    sem_u = sem("s_u")
    sem_u2 = sem("s_u2")
    sem_cc = sem("s_cc")
    sem_castall = sem("s_castall")

    cx = Ctx()
    J = cx.job

    def X(c):
        return ("x", c)

    def HN(c):
        return ("hn", c)

    def H(b):
        return ("H", b)

    def A(b):
        return ("A", b)

    def AK(lo, hi):
        return [A(b) for b in range(lo // 512, (hi - 1) // 512 + 1)]

    def PS(b):
        return ("ps", b)

    def W(s):
        return ("W", s)

    XALL = [X(c) for c in range(16)]
    HNALL = [HN(c) for c in range(16)]
    UKEYS = AK(0, 4224)

    bank_ctr = [0]

    def next_bank():
        b = bank_ctr[0] % 7
        bank_ctr[0] += 1
        return b

    slot_ctr = [0]

    def next_slot():
        s = slot_ctr[0] % 2
        slot_ctr[0] += 1
        return s

    def F_act(out, in_, func, **kw):
        return lambda e: e.activation(out=out, in_=in_, func=func, **kw)

    def F_tt(out, in0, in1, op):
        return lambda e: e.tensor_tensor(out=out, in0=in0, in1=in1, op=op)

    def F_stt(out, in0, scalar, in1, op0, op1):
        return lambda e: e.scalar_tensor_tensor(out=out, in0=in0, scalar=scalar, in1=in1, op0=op0, op1=op1)

    def F_copy(out, in_):
        return lambda e: e.tensor_copy(out=out, in_=in_)

    def F_memset(out, v):
        return lambda e: e.memset(out, v)

    def F_recip(out, in_):
        return lambda e: e.reciprocal(out=out, in_=in_)

    def F_mm(groups):
        def fn(e):
            ins = None
            for out, pairs, sf, sl in groups:
                n = len(pairs)
                for i, (lt, rh) in enumerate(pairs):
                    ins = e.matmul(out, lhsT=lt, rhs=rh, start=(sf and i == 0), stop=(sl and i == n - 1))
            return ins
        return fn

    def F_tr(items):
        def fn(e):
            ins = None
            for out, in_, idn in items:
                ins = e.transpose(out=out, in_=in_, identity=idn)
            return ins
        return fn

    def F_dma(items, s):
        def fn(e):
            ins = None
            for out, in_ in items:
                ins = e.dma_start(out=out, in_=in_)
                ins.then_inc(s, 16)
            return ins
        return fn

    J(st_gp, [], ["prm"], F_dma([(prmT[:, :], prm[:, :])], sem_misc), dma=(sem_misc, 1))
    J(st_dve, [], ["ones"], F_memset(onesm[:, :], 1.0 / 2048.0))
    J(st_dve, [], [("vp", kb) for kb in range(12)], F_memset(Vp[:, :, :, :], 1.0))
    J(st_dve, [], [A(b) for b in range(16)], F_memset(bufA[:, :], 0.0))
    J(st_dve, [], ["zt"], F_memset(zt[:, :], 0.0))
    J(st_dve, [], [("esb", h) for h in range(16)], F_memset(esrow[:, :], 0.0))
    J(st_dve, [], ["zo"], F_memset(zo[:, :], 0.0))
    J(st_dve, ["zo"], ["zo"], F_memset(zo[0:1, 64:128], 1.0))

    cc_items = []
    for l in range(2):
        cc_items.append((kso[l, :, 0:64, :], ck[l, :, 64:128, :]))
        cc_items.append((vso[l, :, 0:64, :], cv[l, :, 64:128, :]))

    CAST_CW = {"in": 1280, "out": 1024, "gate": 1408, "up": 1408}

    def emit_casts():
        dmas = []
        for l in range(2):
            def pieces(n):
                src, dst = w_src[n], w_bf[n]
                rows, cols, cw = src.shape[1], src.shape[2], CAST_CW[n]
                return [[(dst[l, r0:min(rows, r0 + 512), pi * cw:(pi + 1) * cw],
                          src[l, r0:min(rows, r0 + 512), pi * cw:(pi + 1) * cw]) for r0 in range(0, rows, 512)]
                        for pi in range(cols // cw)]
            seq = []
            pin, pout, pg, pu = pieces("in"), pieces("out"), pieces("gate"), pieces("up")
            seq += [("in", 0, pin[0]), ("in", 1, pin[1])]
            seq += [("pw", 0, [(w_bf["pw"][l, :, :], w_src["pw"][l, :, :])])]
            seq += [("out", 0, pout[0]), ("out", 1, pout[1])]
            for pi in range(4):
                seq += [("gate", pi, pg[pi]), ("up", pi, pu[pi])]
            for ri, r0 in enumerate(range(0, DFF, 512)):
                seq += [("down", ri, [(w_bf["down"][l, r0:r0 + 512, :], w_src["down"][l, r0:r0 + 512, :])])]
            for n, pi, items in seq:
                for k, it in enumerate(items):
                    dmas.append((it, [("Wb", l, n, pi)] if k == len(items) - 1 else []))
        prev = None
        for i in range(0, len(dmas), 2):
            grp = dmas[i:i + 2]
            keys = [k for g in grp for k in g[1]]
            pk = ("castpair", i)
            J(st_gp, [prev] if prev else [], keys + [pk], F_dma([g[0] for g in grp], sem_castall),
              dma=(sem_castall, len(grp)))
            prev = pk

    wp_last = {}

    def cast_keys_for(l, n, k0, nk, c0):
        if n == "pw":
            return [("Wb", l, "pw", 0)]
        if n == "down":
            return [("Wb", l, "down", ri) for ri in range((k0 * 128) // 512, ((k0 + nk) * 128 - 1) // 512 + 1)]
        cw = CAST_CW[n]
        return [("Wb", l, n, pi) for pi in range(c0 // cw, (c0 + 511) // cw + 1)]

    def load_w(l, n, src_ap, dst_fn):
        s = next_slot()
        keys = cast_keys_for(*wp_last.pop("key", (l, n, 0, 0, 0)))
        J(st_sp, keys, [W(s)], F_dma([(dst_fn(s), src_ap)], sem_w[s]), dma=(sem_w[s], 1))
        return s

    class Rms:
        def __init__(self, T):
            self.T, self.b, self.n, self.pend = T, 7, 0, []

        def acc(self, sq_chunk_ap, key):
            c = self.n
            self.n += 1
            J(st_pe, [key, "ones"], [PS(self.b)],
              F_mm([(ps[:, self.b, 0:self.T], [(onesm[:, :], sq_chunk_ap)], c == 0, c == 15)]))

        def acc_delayed(self, sq_chunk_ap, key, delay=2):
            self.pend.append((sq_chunk_ap, key))
            if len(self.pend) > delay:
                self.acc(*self.pend.pop(0))

        def finish(self):
            for it in self.pend:
                self.acc(*it)
            self.pend = []
            T, b = self.T, self.b
            J(st_act, [PS(b)], ["rstd"], F_act(rstd[:, 0:T], ps[:, b, 0:T], AF.Ln, bias=EPS))
            J(st_act, ["rstd"], ["rstd"], F_act(rstd[:, 0:T], rstd[:, 0:T], AF.Exp, scale=-0.5))

    POOL_RES = (11, 12, 13, 14, 15)
    POOL_HN = (13, 14, 15)

    def hn_from_x(T, gcol, use_pool=False):
        for c in range(16):
            if use_pool and c in POOL_HN:
                J(st_gp, [X(c), "prm"], ["tmpP"],
                  lambda e, c=c: e.tensor_scalar(out=tmpP[:, 0:T], in0=xT[:, c, 0:T],
                                                 scalar1=prmT[:, gcol + c:gcol + c + 1], scalar2=None, op0=ALU.mult))
                J(st_gp, ["tmpP", "rstd"], [HN(c)], F_tt(hn[:, c, 0:T], tmpP[:, 0:T], rstd[:, 0:T], ALU.mult))
        for c in range(16):
            if use_pool and c in POOL_HN:
                continue
            J(st_dve, [X(c), "rstd", "prm"], [HN(c)],
              F_stt(hn[:, c, 0:T], xT[:, c, 0:T], prmT[:, gcol + c:gcol + c + 1], rstd[:, 0:T], ALU.mult, ALU.mult))

    def prenorm_big(T, gcol):
        J(st_act, XALL, [H(b) for b in range(28, 44)], F_act(sqH[:, :, 0:T], xT[:, :, 0:T], AF.Square))
        r = Rms(T)
        for c in range(16):
            r.acc(sqH[:, c, 0:T], H(28 + c))
        r.finish()
        hn_from_x(T, gcol)

    def residual_update(T, src, next_gcol, use_pool):
        r = Rms(T) if next_gcol is not None else None
        for c in range(16):
            st_e = st_gp if (use_pool and c in POOL_RES) else st_dve
            J(st_e, [A(c), "rstd"], [A(c)], F_tt(src[:, c, 0:T], src[:, c, 0:T], rstd[:, 0:T], ALU.mult))
            J(st_e, [A(c), X(c)], [X(c)], F_tt(xT[:, c, 0:T], xT[:, c, 0:T], src[:, c, 0:T], ALU.add))
            if r is not None:
                J(st_act, [X(c)], [H(28 + c)], F_act(sqH[:, c, 0:T], xT[:, c, 0:T], AF.Square))
                r.acc(sqH[:, c, 0:T], H(28 + c))
        if r is not None:
            r.finish()
            hn_from_x(T, next_gcol, use_pool)

    def mm_trailing(out_ap, slot_key, pairs, bank):
        n = len(pairs)
        for kc, (lt, rh) in enumerate(pairs):
            J(st_pe, [slot_key, HN(kc)], [PS(bank)], F_mm([(out_ap, [(lt, rh)], kc == 0, kc == n - 1)]))

    def wpanel_src(l, n, k0, nk, c0):
        wp_last["key"] = (l, n, k0, nk, c0)
        return w_bf[n][l].rearrange("(kc p) c -> p kc c", p=128)[:, k0:k0 + nk, c0:c0 + 512]

    def tile_layer(tile, l, after_xload=None, use_pool=False):
        kind, T = tile["kind"], tile["T"]
        NCH = T // 64
        NTB = T // 128
        tno = tile.get("t", 0)
        is_p = kind == "P"
        last_p = is_p and tno == 3

        def seg(ap2d_T):
            return ap2d_T.rearrange("p (s t) -> p s t", s=4)

        if l == 0:
            for tb in range(NTB):
                src = xp[tno * 512 + tb * 128: tno * 512 + (tb + 1) * 128, :] if is_p else xs[tb * 128:(tb + 1) * 128, :]
                skeys = xstg_keys[tb]
                J(st_sp, [], skeys, F_dma([(xstg[tb], src)], sem_xl[tb]), dma=(sem_xl[tb], 1))
            for tb in range(NTB):
                skeys = xstg_keys[tb]
                for q in range(4):
                    b = next_bank()
                    J(st_pe, skeys + ["prm"], [PS(b)],
                      F_tr([(ps[:, b, j * 128:(j + 1) * 128], xstg[tb][:, (4 * q + j) * 128:(4 * q + j + 1) * 128], ident)
                            for j in range(4)]))
                    J(st_act, [PS(b)], [X(4 * q + j) for j in range(4)],
                      F_act(xT[:, 4 * q:4 * q + 4, tb * 128:(tb + 1) * 128],
                            ps[:, b, :].rearrange("p (j t) -> p j t", j=4), AF.Copy))

        if after_xload is not None:
            after_xload()

        J(st_act, ["prm"], ["es16"], F_act(es16[:, :], prmT[:, PC_SK + 16 * l:PC_SK + 16 * l + 16], AF.Exp))
        for h in range(16):
            J(st_dve, ["es16", "zt"], [("esb", h)],
              lambda e, h=h: e.tensor_scalar(out=esrow[0:1, h * 64:(h + 1) * 64], in0=zt[0:1, :],
                                             scalar1=es16[0:1, h:h + 1], scalar2=None, op0=ALU.add))

        if is_p:
            if tno == 0:
                J(st_dve, [], UKEYS, F_memset(U[:, :, 0:16], 0.0))
            else:
                J(st_act, [("kth", l)], [("kt", 0), ("kt", 1)], F_act(KT[:, :, 0:128], KTh[l][:, :, :], AF.Copy))
                J(st_act, [("vh", l)], [("vp", 0), ("vp", 1)], F_act(Vp[:, 0:2, :, :], Vh[l][:, :, :, :], AF.Copy))
                J(st_act, [("uh", l)], UKEYS, F_act(U[:, :, 0:16], Uh[l][:, :, :], AF.Copy))
        else:
            J(st_gp, [], AK(2560, 3584), F_dma([(ckst, ck[l].rearrange("s t f -> t s f"))], sem_ck), dma=(sem_ck, 1))
            for gs in range(2):
                b = next_bank()
                J(st_pe, AK(2560, 3584) + ["prm"], [PS(b)],
                  F_tr([(ps[:, b, s_ * 128:(s_ + 1) * 128], ckst[:, s_, gs * 128:(gs + 1) * 128], ident) for s_ in range(4)]))
                J(st_act, [PS(b)], [("kt", gs)],
                  F_act(KT[:, gs, :].rearrange("p (s e) -> p s e", s=4)[:, :, 0:128],
                        ps[:, b, :].rearrange("p (s t) -> p s t", s=4), AF.Copy))
            items = []
            for s_ in range(4):
                for k2 in range(2):
                    items.append((Vp[0:64, 3 * s_ + k2, :, 0:64],
                                  cv[l, s_, k2 * 64:(k2 + 1) * 64, :].rearrange("p (g d) -> p g d", g=4)))
            J(st_gp, [], [("vp", 3 * s_ + k2) for s_ in range(4) for k2 in range(2)], F_dma(items, sem_cv),
              dma=(sem_cv, len(items)))
            J(st_gp, [], AK(7168, 8192), F_dma([(spst, sp[l, :, :])], sem_sp), dma=(sem_sp, 1))
            b = next_bank()
            J(st_pe, AK(7168, 8192) + ["prm"], [PS(b)],
              F_tr([(ps[:, b, c * 60:(c + 1) * 60], spst[:, c * 128:(c + 1) * 128], prmT[0:60, PC_ID:PC_ID + 60])
                    for c in range(8)]))
            for c in range(8):
                J(st_act, [PS(b)], AK(c * 528, (c + 1) * 528),
                  F_act(U[:, c, 0:320].rearrange("p (s e) -> p s e", s=4)[:, :, 1:16],
                        ps[:, b, c * 60:(c + 1) * 60].rearrange("p (s r) -> p s r", s=4), AF.Copy))

        if l == 0:
            prenorm_big(T, PC_GMP + 16 * l)

        def u_out(c):
            if is_p:
                return U[:, c, 16:16 + T]
            return U[:, c, 0:320].rearrange("p (s e) -> p s e", s=4)[:, :, 16:80]

        def psT(b):
            return ps[:, b, 0:T] if is_p else seg(ps[:, b, 0:T])

        def pooling():
            for g, w in enumerate(POOL_WINS):
                if is_p:
                    views = [(U[:, 2 * g:2 * g + 2, :], tmpA.rearrange("p (c e) -> p c e", c=2),
                              tmpB.rearrange("p (c e) -> p c e", c=2), dT[:, 2 * g:2 * g + 2, 0:T], 528,
                              AK(2 * g * 528, (2 * g + 2) * 528), [H(16 + 2 * g), H(17 + 2 * g)], 2 * g)]
                else:
                    views = []
                    for c in (2 * g, 2 * g + 1):
                        views.append((U[:, c, 0:320].rearrange("p (s e) -> p s e", s=4),
                                      tmpA[:, 0:320].rearrange("p (s e) -> p s e", s=4),
                                      tmpB[:, 0:320].rearrange("p (s e) -> p s e", s=4),
                                      seg(dT[:, c, 0:T]), 80, AK(c * 528, (c + 1) * 528), [H(16 + c)], c))
                for V_, tA, tB, dO, E, ukeys, dkeys, c0 in views:
                    J(st_dve, ukeys, ["tA"], F_tt(tA[:, :, 1:E], V_[:, :, 1:E], V_[:, :, 0:E - 1], ALU.add))
                    fin, oth, fk, ok = tA, tB, "tA", "tB"
                    if w >= 4:
                        J(st_dve, ["tA"], ["tB"], F_tt(tB[:, :, 3:E], tA[:, :, 3:E], tA[:, :, 1:E - 2], ALU.add))
                        fin, oth, fk, ok = tB, tA, "tB", "tA"
                    if w >= 8:
                        J(st_dve, ["tB"], ["tA"], F_tt(tA[:, :, 7:E], tB[:, :, 7:E], tB[:, :, 3:E - 4], ALU.add))
                        fin, oth, fk, ok = tA, tB, "tA", "tB"
                    if w >= 16:
                        J(st_dve, ["tA"], ["tB"], F_tt(tB[:, :, 15:E], tA[:, :, 15:E], tA[:, :, 7:E - 8], ALU.add))
                        fin, oth, fk, ok = tB, tA, "tB", "tA"
                    J(st_dve, [fk] + ukeys, dkeys,
                      F_stt(dO, fin[:, :, 16:E], 1.0 / w, V_[:, :, 16:E], ALU.mult, ALU.subtract))
                    if is_p and tno == 0:
                        rc = prmT[:, PC_RC + c0 * 16:PC_RC + (c0 + 2) * 16].rearrange("p (c t) -> p c t", c=2)
                        J(st_dve, [fk, "prm"], [ok], F_tt(oth[:, :, 0:16], fin[:, :, 16:32], rc, ALU.mult))
                        J(st_dve, [ok] + ukeys, dkeys, F_tt(dO[:, :, 0:16], oth[:, :, 0:16], V_[:, :, 16:32], ALU.subtract))
            if is_p and tno < 3:
                J(st_act, UKEYS, [("uh", l)], F_act(Uh[l][:, :, :], U[:, :, 512:528], AF.Copy))


        u_slots = []
        for panel in range(5):
            s = load_w(l, "in", wpanel_src(l, "in", 0, 16, panel * 512), lambda s_: wsl[s_][:, :, :])
            if panel < 2:
                u_slots.append(s)
            nchunk = 4 if panel < 4 else 2
            for m in range(nchunk):
                b = next_bank()
                prs = [(wsl[s][:, kc, m * 128:(m + 1) * 128], hn[:, kc, 0:T]) for kc in range(16)]
                if panel == 0 and m == 0:
                    mm_trailing(ps[:, b, 0:T], W(s), prs, b)
                else:
                    J(st_pe, [W(s)] + HNALL, [PS(b)], F_mm([(ps[:, b, 0:T], prs, True, True)]))
                if panel < 2:
                    c = panel * 4 + m
                    J(st_act, [PS(b)], AK(c * 528, (c + 1) * 528), F_act(u_out(c), psT(b), AF.Copy))
                elif panel < 4:
                    i = (panel - 2) * 4 + m
                    J(st_act, [PS(b)], [H(24 + i)], F_act(QT[:, i, 0:T], ps[:, b, 0:T], AF.Copy, scale=0.125))
                else:
                    gs = m
                    if is_p:
                        ko = KT[:, gs, 128:128 + T]
                    else:
                        ko = KT[:, gs, :].rearrange("p (s e) -> p s e", s=4)[:, :, 128:192]
                    J(st_act, [PS(b)], [("kt", gs)], F_act(ko, psT(b), AF.Copy))
            if panel == 1 and (last_p or not is_p):
                if is_p:
                    rounds = [([U[:, c, 16 + T - 15:16 + T] for c in range(8)], ppo[l, :, :])]
                else:
                    rounds = [([U[:, c, s_ * 80 + 65:s_ * 80 + 80] for c in range(8)], pso[l, s_, :, :]) for s_ in range(4)]
                for ri, (srcs, dst) in enumerate(rounds):
                    base = 7168 if ri % 2 == 0 else 5632
                    bk = base // 512
                    for half in range(2):
                        b = next_bank()
                        J(st_pe, AK(half * 4 * 528, (half * 4 + 4) * 528) + ["prm"], [PS(b)],
                          F_tr([(ps[0:15, b, j * 128:(j + 1) * 128], srcs[half * 4 + j], ident) for j in range(4)]))
                        J(st_dve, [PS(b)], [A(bk + half)],
                          F_copy(bufA[0:15, base + half * 512:base + (half + 1) * 512], ps[0:15, b, :]))
                    smu = sem_u if ri % 2 == 0 else sem_u2
                    J(st_gp, [A(bk), A(bk + 1)], [], F_dma([(dst, bufA[0:15, base:base + 1024])], smu), dma=(smu, 1))
            if panel == 1:
                pooling()
            if panel == 4:
                def needK_of(cq_):
                    return (last_p and cq_ >= 6) or (not is_p)
                skip = set()
                for cq in range(NCH):
                    if cq in skip:
                        continue
                    needK = needK_of(cq)
                    if is_p and (not needK) and cq + 1 < NCH and not needK_of(cq + 1):
                        skip.add(cq + 1)
                        b = next_bank()
                        J(st_pe, [W(s)] + HNALL, [PS(b)],
                          F_mm([(ps[:, b, 0:256],
                                 [(hn[:, kc, cq * 64:(cq + 2) * 64], wsl[s][:, kc, 256:512]) for kc in range(16)], True, True)]))
                        for hh in range(2):
                            J(st_dve, [PS(b)], [("vp", 2 + cq + hh)],
                              F_copy(Vp[0:64, 2 + cq + hh, :, 0:64],
                                     ps[hh * 64:(hh + 1) * 64, b, 0:256].rearrange("p (g d) -> p g d", g=4)))
                        continue
                    kb_own = (2 + cq) if is_p else (3 * cq + 2)
                    b = next_bank()
                    c0 = 0 if needK else 256
                    N = 512 - c0
                    J(st_pe, [W(s)] + HNALL, [PS(b)],
                      F_mm([(ps[0:64, b, 0:N],
                             [(hn[:, kc, cq * 64:(cq + 1) * 64], wsl[s][:, kc, c0:512]) for kc in range(16)], True, True)]))
                    J(st_dve, [PS(b)], [("vp", kb_own)],
                      F_copy(Vp[0:64, kb_own, :, 0:64], ps[0:64, b, N - 256:N].rearrange("p (g d) -> p g d", g=4)))
                    if needK:
                        J(st_act, [PS(b)], [A(13)], F_act(kvst[:, :], ps[0:64, b, 0:512], AF.Copy))
                        if is_p:
                            r0 = (cq - 6) * 64
                            items = [(kpo[l, r0:r0 + 64, :], kvst[:, 0:256]), (vpo[l, r0:r0 + 64, :], kvst[:, 256:512])]
                        else:
                            items = [(kso[l, cq, 64:128, :], kvst[:, 0:256]), (vso[l, cq, 64:128, :], kvst[:, 256:512])]
                        J(st_gp, [A(13)], [], F_dma(items, sem_kv), dma=(sem_kv, 2))

        s = load_w(l, "pw", w_bf["pw"][l].rearrange("(j p) c -> p j c", p=128),
                   lambda s_: wsl[s_][:, 0:4, :].rearrange("p a (b c) -> p (a b) c", c=256))
        pwv = wsl[s][:, 0:4, :].rearrange("p a (b c) -> p (a b) c", c=256)
        for g in range(4):
            for mo in range(2):
                b = next_bank()
                J(st_pe, [W(s), H(16 + 2 * g), H(17 + 2 * g)], [PS(b)],
                  F_mm([(ps[:, b, 0:T], [(pwv[:, 2 * g + ki, mo * 128:(mo + 1) * 128], dT[:, 2 * g + ki, 0:T])
                                         for ki in range(2)], True, True)]))
                cc = 2 * g + mo
                J(st_act, [PS(b), "prm"], [H(cc)],
                  F_act(catT[:, cc, 0:T], ps[:, b, 0:T], AF.Copy, scale=prmT[:, PC_PS + 8 * l + cc:PC_PS + 8 * l + cc + 1]))

        jobs = []
        for c in range(NCH):
            if is_p:
                kbs = [kb for kb in (c, c + 1, c + 2) if not (tno == 0 and kb < 2)]
            else:
                kbs = [3 * c, 3 * c + 1, 3 * c + 2]
            for gs in range(2):
                for hf in range(2):
                    jobs.append((c, gs, hf, kbs))

        def qk(j):
            c, gs, hf, kbs = jobs[j]
            nk = len(kbs)
            base = 2 * (j % 3)
            groups = []
            for ki, kb in enumerate(kbs):
                o0 = base * 512 + ki * 256
                groups.append((psf[0:64, o0:o0 + 256],
                               [(KT[hf * 64:(hf + 1) * 64, gs, kb * 64:(kb + 1) * 64],
                                 QT[hf * 64:(hf + 1) * 64, 4 * gs:4 * gs + 4, c * 64:(c + 1) * 64])], True, True))
            J(st_pe, [("kt", gs)] + [H(24 + 4 * gs + i) for i in range(4)], [PS(base), PS(base + 1)], F_mm(groups))
            n = nk * 256
            J(st_act, [PS(base), PS(base + 1)], [("pt", j % 3)],
              F_act(PT[j % 3][:, 0:n], psf[0:64, base * 512:base * 512 + n], AF.Exp))

        def pv(j):
            c, gs, hf, kbs = jobs[j]
            nk = len(kbs)
            bo = 6 + (j % 2)
            g = 2 * gs + hf
            e0 = (2 * gs + hf) * 256
            groups = [(ps[:, bo, 0:256],
                       [(Vp[0:64, kb, g, :], PT[j % 3][:, ki * 256:(ki + 1) * 256]) for ki, kb in enumerate(kbs)]
                       + [(zo[:, :], esrow[:, e0:e0 + 256])],
                       True, True)]
            J(st_pe, [("pt", j % 3), "zo"] + [("vp", kb) for kb in kbs]
              + [("esb", h) for h in range(8 * gs + 4 * hf, 8 * gs + 4 * hf + 4)], [PS(bo)], F_mm(groups))
            ta, tb_ = t1[j % 2], t2[j % 2]
            if (j % 5) in (1, 3):
                J(st_dve, [PS(bo)], [("t2", j % 2)], F_recip(tb_[0:64, 0:256], ps[64:128, bo, 0:256]))
            else:
                J(st_act, [PS(bo)], ["t1"], F_act(ta[64:128, 0:256], ps[64:128, bo, 0:256], AF.Ln))
                J(st_act, ["t1"], [("t2", j % 2)], F_act(tb_[0:64, 0:256], ta[64:128, 0:256], AF.Exp, scale=-1.0))
            J(st_dve, [PS(bo), ("t2", j % 2)], [H(8 + 4 * gs + i) for i in range(4)],
              F_tt(catT[hf * 64:(hf + 1) * 64, 8 + 4 * gs:12 + 4 * gs, c * 64:(c + 1) * 64],
                   ps[0:64, bo, 0:256].rearrange("p (i q) -> p i q", i=4),
                   tb_[0:64, 0:256].rearrange("p (i q) -> p i q", i=4), ALU.mult))

        nj = len(jobs)
        qk(0)
        if nj > 1:
            qk(1)
        for j in range(nj):
            if j + 2 < nj:
                qk(j + 2)
            pv(j)
        if is_p and tno < 3:
            J(st_act, [("kt", 0), ("kt", 1)], [("kth", l)], F_act(KTh[l][:, :, :], KT[:, :, 512:640], AF.Copy))
            J(st_act, [("vp", 8), ("vp", 9)], [("vh", l)], F_act(Vh[l][:, :, :, :], Vp[:, 8:10, :, :], AF.Copy))

        gcol = PC_GMO + 16 * l
        rm = Rms(T)
        for panel in range(4):
            s = load_w(l, "out", wpanel_src(l, "out", 0, 16, panel * 512), lambda s_: wsl[s_][:, :, :])
            for m in range(4):
                co = 4 * panel + m
                b = next_bank()
                J(st_pe, [W(s)] + [H(k) for k in range(16)], [PS(b)],
                  F_mm([(ps[:, b, 0:T], [(wsl[s][:, kc, m * 128:(m + 1) * 128], catT[:, kc, 0:T]) for kc in range(16)],
                         True, True)]))
                J(st_act, [PS(b)], [H(28 + co)], F_act(sqH[:, co, 0:T], ps[:, b, 0:T], AF.Square))
                J(st_act, [PS(b), "prm"], [A(co)],
                  F_act(mix[:, co, 0:T], ps[:, b, 0:T], AF.Copy, scale=prmT[:, gcol + co:gcol + co + 1]))
                rm.acc_delayed(sqH[:, co, 0:T], H(28 + co))
        rm.finish()
        residual_update(T, mix, PC_GFP + 16 * l, use_pool)

        for grp in range(11):
            s_g = load_w(l, "gate", wpanel_src(l, "gate", 0, 16, grp * 512), lambda s_: wsl[s_][:, :, :])
            s_u = load_w(l, "up", wpanel_src(l, "up", 0, 16, grp * 512), lambda s_: wsl[s_][:, :, :])
            for m in range(4):
                b = next_bank()
                prs = [(wsl[s_g][:, kc, m * 128:(m + 1) * 128], hn[:, kc, 0:T]) for kc in range(16)]
                if grp == 0 and m == 0:
                    mm_trailing(ps[:, b, 0:T], W(s_g), prs, b)
                else:
                    J(st_pe, [W(s_g)] + HNALL, [PS(b)], F_mm([(ps[:, b, 0:T], prs, True, True)]))
                J(st_act, [PS(b)], [("sg", m)], F_act(sg[m][:, 0:T], ps[:, b, 0:T], AF.Silu))
            for m in range(4):
                k = 4 * grp + m
                b = next_bank()
                J(st_pe, [W(s_u)] + HNALL, [PS(b)],
                  F_mm([(ps[:, b, 0:T], [(wsl[s_u][:, kc, m * 128:(m + 1) * 128], hn[:, kc, 0:T]) for kc in range(16)],
                         True, True)]))
                J(st_dve, [PS(b), ("sg", m)], [H(k)], F_tt(hid[:, k, 0:T], sg[m][:, 0:T], ps[:, b, 0:T], ALU.mult))
        gcol = PC_GFO + 16 * l
        rm = Rms(T)
        for panel in range(4):
            banks = [next_bank() for _ in range(4)]
            for (k0, nk) in ((0, 16), (16, 16), (32, 12)):
                s = load_w(l, "down", wpanel_src(l, "down", k0, nk, panel * 512),
                           lambda s_, nk=nk: wsl[s_][:, 0:nk, :])
                for m in range(4):
                    b = banks[m]
                    J(st_pe, [W(s)] + [H(k0 + jj) for jj in range(nk)], [PS(b)],
                      F_mm([(ps[:, b, 0:T],
                             [(wsl[s][:, jj, m * 128:(m + 1) * 128], hid[:, k0 + jj, 0:T]) for jj in range(nk)],
                             k0 == 0, k0 + nk == 44)]))
            for m in range(4):
                co = 4 * panel + m
                b = banks[m]
                J(st_act, [PS(b)], [HN(co)], F_act(hn[:, co, 0:T], ps[:, b, 0:T], AF.Square))
                J(st_act, [PS(b), "prm"], [A(co)],
                  F_act(mix[:, co, 0:T], ps[:, b, 0:T], AF.Copy, scale=prmT[:, gcol + co:gcol + co + 1]))
                rm.acc_delayed(hn[:, co, 0:T], HN(co))
        rm.finish()
        residual_update(T, mix, (PC_GMP + 16 * (l + 1)) if l + 1 < NL else None, use_pool)

        if l == NL - 1:
            for tb in range(NTB):
                k = tb % 2
                for q in range(4):
                    b = next_bank()
                    J(st_pe, [X(4 * q + j) for j in range(4)] + ["prm"], [PS(b)],
                      F_tr([(ps[:, b, j * 128:(j + 1) * 128], xT[:, 4 * q + j, tb * 128:(tb + 1) * 128], ident)
                            for j in range(4)]))
                    J(st_act, [PS(b)], [HN(8 * k + 2 * q), HN(8 * k + 2 * q + 1)],
                      F_act(stg[k][:, q * 512:(q + 1) * 512], ps[:, b, :], AF.Copy))
                dst = yp[tno * 512 + tb * 128: tno * 512 + (tb + 1) * 128, :] if is_p else ys[tb * 128:(tb + 1) * 128, :]
                J(st_gp, [HN(8 * k + i) for i in range(8)], [], F_dma([(dst, stg[k])], sem_ys[k]), dma=(sem_ys[k], 1))

    tiles = [dict(kind="P", t=t, T=512) for t in range(4)] + [dict(kind="S", T=256)]

    def prologue_dma():
        J(st_gp, [], [], F_dma(cc_items, sem_cc), dma=(sem_cc, len(cc_items)))
        emit_casts()

    if tile_sel is not None:
        tiles = [tiles[i] for i in tile_sel]
    for ti, tile in enumerate(tiles):
        for l in range(NL):
            tile_layer(tile, l, after_xload=prologue_dma if (ti == 0 and l == 0) else None, use_pool=False)

    finals = []
    for sm in (sem_ys[0], sem_ys[1], sem_kv, sem_u, sem_u2, sem_cc):
        v = cx.dma_cnt.get(id(sm), 0)
        if v:
            finals.append((sm, v))

    with nc.Block() as block:
        @block.tensor
        def _(e):
            _run(st_pe, e)

        @block.scalar
        def _(e):
            _run(st_act, e)

        @block.vector
        def _(e):
            _run(st_dve, e)

        @block.sync
        def _(e):
            _run(st_sp, e)

        @block.gpsimd
        def _(e):
            _run(st_gp, e)
            for sm, v in finals:
                e.wait_ge(sm, v)

    es.close()
    return nc


_NC_CACHE = {}


def _get_nc():
    if "nc" not in _NC_CACHE:
        _NC_CACHE["nc"] = build_nc()
    return _NC_CACHE["nc"]


def _prep_params(inp):
    prm = np.zeros((128, NPRM), np.float32)

    def fm(g):
        return np.ascontiguousarray(np.asarray(g, np.float32).reshape(16, 128).T)

    for l in range(NL):
        prm[:, PC_GMP + 16 * l:PC_GMP + 16 * l + 16] = fm(inp["g_mix_pre"][l])
        prm[:, PC_GMO + 16 * l:PC_GMO + 16 * l + 16] = fm(inp["g_mix_post"][l])
        prm[:, PC_GFP + 16 * l:PC_GFP + 16 * l + 16] = fm(inp["g_ffn_pre"][l])
        prm[:, PC_GFO + 16 * l:PC_GFO + 16 * l + 16] = fm(inp["g_ffn_post"][l])
        prm[:, PC_PS + 8 * l:PC_PS + 8 * l + 8] = np.asarray(inp["pool_scale"][l], np.float32).reshape(8, 128).T
        sk = np.asarray(inp["attn_sinks"][l], np.float32)
        order = [A_IDX[4 * gs + i] + 4 * hf for gs in range(2) for hf in range(2) for i in range(4)]
        prm[:, PC_SK + 16 * l:PC_SK + 16 * l + 16] = sk[order][None, :]
    rc = np.zeros((8, 16), np.float32)
    for c in range(8):
        w = POOL_WINS[c // 2]
        for t in range(16):
            rc[c, t] = 1.0 / min(t + 1, w)
    prm[:, PC_RC:PC_RC + 128] = rc.reshape(1, 128)
    prm[:, PC_ID:PC_ID + 128] = np.eye(128, dtype=np.float32)
    return prm


def kernel(**inputs):
    inp = {k: np.asarray(v) for k, v in inputs.items()}
    n = 8
    qorder = [A_IDX[i] + 4 * hf for i in range(8) for hf in range(2)]
    w_in = inp["w_in"].astype(np.float32, copy=False)
    qcols = np.concatenate([np.arange(1024 + h * 64, 1024 + (h + 1) * 64) for h in qorder])
    cols = np.concatenate([np.arange(0, 1024), qcols, np.arange(2048, 2560)])
    w_in_p = np.ascontiguousarray(w_in[:, :, cols])
    rows = np.concatenate([np.arange(0, 1024), qcols])
    w_out_p = np.ascontiguousarray(inp["w_out"].astype(np.float32, copy=False)[:, rows, :])
    pool_w = np.ascontiguousarray(inp["pool_w"].astype(np.float32, copy=False).reshape(2, 1024, 256))
    prm = _prep_params(inp)
    shared = {
        "prm": prm, "w_in": w_in_p, "pool_w": pool_w, "w_out": w_out_p,
        "w_gate": np.ascontiguousarray(inp["w_gate"], dtype=np.float32),
        "w_up": np.ascontiguousarray(inp["w_up"], dtype=np.float32),
        "w_down": np.ascontiguousarray(inp["w_down"], dtype=np.float32),
    }
    in_maps = []
    for c in range(n):
        m = dict(shared)
        m["xp"] = np.ascontiguousarray(inp["x_prompt"][c], dtype=np.float32)
        m["xs"] = np.ascontiguousarray(inp["x_sample"][4 * c:4 * c + 4].reshape(256, D), dtype=np.float32)
        m["ck"] = np.ascontiguousarray(inp["cache_k"][:, 4 * c:4 * c + 4].reshape(2, 4, 128, 256), dtype=np.float32)
        m["cv"] = np.ascontiguousarray(inp["cache_v"][:, 4 * c:4 * c + 4].reshape(2, 4, 128, 256), dtype=np.float32)
        m["sp"] = np.ascontiguousarray(inp["state_pool"][:, 4 * c:4 * c + 4].reshape(2, 60, 1024), dtype=np.float32)
        in_maps.append(m)
    nc = _get_nc()
    res = run_bass_kernel_spmd(nc, in_maps, core_ids=list(range(n)))
    R = res.results
    y_prompt = np.stack([np.asarray(R[c]["yp"], np.float32) for c in range(n)], 0)
    y_sample = np.concatenate([np.asarray(R[c]["ys"], np.float32).reshape(4, 64, D) for c in range(n)], 0)
    kp = np.stack([np.asarray(R[c]["kpo"], np.float32).reshape(2, 128, 4, 64) for c in range(n)], 1)
    vp = np.stack([np.asarray(R[c]["vpo"], np.float32).reshape(2, 128, 4, 64) for c in range(n)], 1)
    pp = np.stack([np.asarray(R[c]["ppo"], np.float32) for c in range(n)], 1)
    ks = np.concatenate([np.asarray(R[c]["kso"], np.float32).reshape(2, 4, 128, 4, 64) for c in range(n)], 1)
    vs = np.concatenate([np.asarray(R[c]["vso"], np.float32).reshape(2, 4, 128, 4, 64) for c in range(n)], 1)
    pss = np.concatenate([np.asarray(R[c]["pso"], np.float32) for c in range(n)], 1)
    return (y_prompt, y_sample, kp, vp, pp, ks, vs, pss)
```
